# Optimizing a Trainium2 kernel written in Bass

```python
import math
import jax, jax.numpy as jnp
from jax import lax
import numpy as np

D_MODEL = 1024
BATCH = 16
SEQ = 2048
DEPTH = 1

CTX_LEN = 256
GRID_W = 64

DIFF_HEADS = 8
DIFF_HD = 64
DIFF_QK_W = DIFF_HEADS * 2 * DIFF_HD
DIFF_V_W = DIFF_HEADS * 2 * DIFF_HD
NA_HEADS = 8
NA_HD = 64
NA_W = NA_HEADS * NA_HD
NA_WIN_H = 8
NA_WIN_W = 16
GATE_W = 2 * D_MODEL
Q_COLS = DIFF_QK_W + NA_W
KV_COLS = DIFF_QK_W + DIFF_V_W + 2 * NA_W
IN_W = Q_COLS + GATE_W + KV_COLS
FFN_HIDDEN = ((8 * D_MODEL + 3 * 256 - 1) // (3 * 256)) * 256
N_MOD = 6
Q_BLOCK = 128
ROPE_THETA = 10000.0
LN_EPS = 1e-5
ALPHA = (2.0 * DEPTH) ** 0.25
BETA = (8.0 * DEPTH) ** -0.25

kernel_name = "hybrid_diffattn_natten_dit_layer"


def layer_norm(x, g, b):
    xf = x.astype(jnp.float32)
    mu = jnp.mean(xf, -1, keepdims=True)
    var = jnp.mean(jnp.square(xf - mu), -1, keepdims=True)
    return ((xf - mu) * lax.rsqrt(var + LN_EPS)).astype(x.dtype) * g + b


def rms_norm(x, g):
    xf = x.astype(jnp.float32)
    y = xf * lax.rsqrt(jnp.mean(jnp.square(xf), -1, keepdims=True) + LN_EPS)
    return y.astype(x.dtype) * g


def modulate(x, shift, scale):
    return x * (1 + scale) + shift


def axial_rope_tables(n_tokens, dtype):
    t = jnp.arange(n_tokens)
    row = (t // GRID_W).astype(jnp.float32)
    col = (t % GRID_W).astype(jnp.float32)
    n_freq = DIFF_HD // 4
    inv = ROPE_THETA ** (-jnp.arange(n_freq, dtype=jnp.float32) / n_freq)
    ang = jnp.concatenate([row[:, None] * inv, col[:, None] * inv], axis=-1)
    return jnp.cos(ang).astype(dtype), jnp.sin(ang).astype(dtype)


def apply_axial_rope(x, cos, sin):
    B, L, H, d = x.shape
    xr = x.reshape(B, L, H, 2, 2, d // 4)
    x1, x2 = xr[..., 0, :], xr[..., 1, :]
    c = cos.reshape(L, 1, 2, d // 4)
    s = sin.reshape(L, 1, 2, d // 4)
    out = jnp.stack([x1 * c - x2 * s, x2 * c + x1 * s], axis=-2)
    return out.reshape(B, L, H, d)


def diff_attention(q, k, v, lam, lam_init, subln_g):
    B, Lq, H2, d = q.shape
    H = H2 // 2
    qb_size = min(Q_BLOCK, Lq)
    nb = Lq // qb_size
    scale = d ** -0.5
    qb = jnp.moveaxis(q.reshape(B, nb, qb_size, H2, d), 1, 0)

    def one_block(qi):
        s = jnp.einsum('bqhd,bkhd->bhqk', qi, k).astype(jnp.float32) * scale
        p = jax.nn.softmax(s, axis=-1).reshape(B, H, 2, qb_size, -1)
        a = (p[:, :, 0] - lam * p[:, :, 1]).astype(v.dtype)
        return jnp.einsum('bhqk,bkhe->bqhe', a, v)

    o = jnp.moveaxis(lax.map(one_block, qb), 0, 1).reshape(B, Lq, H, 2 * d)
    o = rms_norm(o, subln_g) * (1.0 - lam_init)
    return o.reshape(B, Lq, H * 2 * d)


def neighbourhood_attention(q, k, v, k_ctx, v_ctx, rpb):
    B, L, H, d = q.shape
    rows = L // GRID_W
    wh = min(NA_WIN_H, rows)
    scale = d ** -0.5
    qg = q.reshape(B, rows, GRID_W, H, d)
    kg = k.reshape(B, rows, GRID_W, H, d)
    vg = v.reshape(B, rows, GRID_W, H, d)
    col = jnp.arange(GRID_W)
    col_start = jnp.clip(col - NA_WIN_W // 2, 0, GRID_W - NA_WIN_W)
    col_ok = (col[None, :] >= col_start[:, None]) & (col[None, :] < col_start[:, None] + NA_WIN_W)
    col_idx = jnp.clip(col[None, :] - col[:, None] + NA_WIN_W - 1, 0, 2 * NA_WIN_W - 2)
    n_loc = wh * GRID_W

    def row_block(r):
        rs = jnp.clip(r - wh // 2, 0, rows - wh)
        kr = lax.dynamic_slice_in_dim(kg, rs, wh, axis=1)
        vr = lax.dynamic_slice_in_dim(vg, rs, wh, axis=1)
        qr = lax.dynamic_index_in_dim(qg, r, axis=1, keepdims=False)
        s_loc = jnp.einsum('bqhd,bjkhd->bhqjk', qr, kr).astype(jnp.float32) * scale
        row_off = rs + jnp.arange(wh) - r
        bias = rpb[:, row_off + NA_WIN_H - 1][:, :, col_idx]
        s_loc = s_loc + jnp.transpose(bias, (0, 2, 1, 3)).astype(jnp.float32)[None]
        s_loc = jnp.where(col_ok[None, None, :, None, :], s_loc, -jnp.inf)
        s_ctx = jnp.einsum('bqhd,bkhd->bhqk', qr, k_ctx).astype(jnp.float32) * scale
        s = jnp.concatenate([s_loc.reshape(B, H, GRID_W, n_loc), s_ctx], axis=-1)
        p = jax.nn.softmax(s, axis=-1).astype(v.dtype)
        p_loc = p[..., :n_loc].reshape(B, H, GRID_W, wh, GRID_W)
        p_ctx = p[..., n_loc:]
        return (jnp.einsum('bhqjk,bjkhd->bqhd', p_loc, vr)
                + jnp.einsum('bhqk,bkhd->bqhd', p_ctx, v_ctx))

    o = lax.map(row_block, jnp.arange(rows))
    return jnp.moveaxis(o, 0, 1).reshape(B, L, H * d)


def context_attention(q, k, v):
    B, Lq, H, d = q.shape
    s = jnp.einsum('bqhd,bkhd->bhqk', q, k).astype(jnp.float32) * d ** -0.5
    p = jax.nn.softmax(s, axis=-1).astype(v.dtype)
    return jnp.einsum('bhqk,bkhd->bqhd', p, v).reshape(B, Lq, H * d)


def split_queries_gates(p):
    B, L = p.shape[:2]
    q_d = p[..., :DIFF_QK_W].reshape(B, L, 2 * DIFF_HEADS, DIFF_HD)
    q_n = p[..., DIFF_QK_W:Q_COLS].reshape(B, L, NA_HEADS, NA_HD)
    gate_logits = p[..., Q_COLS:Q_COLS + GATE_W]
    return q_d, q_n, gate_logits


def split_kv(p):
    B, L = p.shape[:2]
    k_d = p[..., :DIFF_QK_W].reshape(B, L, 2 * DIFF_HEADS, DIFF_HD)
    v_d = p[..., DIFF_QK_W:DIFF_QK_W + DIFF_V_W].reshape(B, L, DIFF_HEADS, 2 * DIFF_HD)
    o = DIFF_QK_W + DIFF_V_W
    k_n = p[..., o:o + NA_W].reshape(B, L, NA_HEADS, NA_HD)
    v_n = p[..., o + NA_W:].reshape(B, L, NA_HEADS, NA_HD)
    return k_d, v_d, k_n, v_n


def merge_branches(o_d, o_n, gate_logits, b_gate, w_branch_diff, w_branch_na, w_out):
    g = jax.nn.sigmoid(gate_logits + b_gate)
    y = g[..., :D_MODEL] * (o_d @ w_branch_diff) + g[..., D_MODEL:] * (o_n @ w_branch_na)
    return y @ w_out


def swiglu(h, w_ffn_in, w_ffn_out):
    gu = h @ w_ffn_in
    return (jax.nn.silu(gu[..., :FFN_HIDDEN]) * gu[..., FFN_HIDDEN:]) @ w_ffn_out


def hybrid_layer(x, ctx, c_silu, c_ctx_silu, w_mod, b_mod, w_in, b_gate, lam_q1, lam_k1, lam_q2,
                 lam_k2, subln_g, na_rpb, w_branch_diff, w_branch_na, w_out, ln1_g, ln1_b,
                 w_ffn_in, w_ffn_out, ln2_g, ln2_b, layer_idx, update_ctx):
    B, L, D = x.shape
    mod_x = (c_silu @ w_mod + b_mod).reshape(B, 1, N_MOD, D)
    mod_c = (c_ctx_silu @ w_mod + b_mod).reshape(1, 1, N_MOD, D)
    sh1, sc1, g1, sh2, sc2, g2 = [mod_x[:, :, i] for i in range(N_MOD)]
    csh1, csc1, cg1, csh2, csc2, cg2 = [mod_c[:, :, i] for i in range(N_MOD)]

    lam_init = 0.8 - 0.6 * math.exp(-0.3 * layer_idx)
    lam = (jnp.exp(jnp.sum(lam_q1.astype(jnp.float32) * lam_k1.astype(jnp.float32)))
           - jnp.exp(jnp.sum(lam_q2.astype(jnp.float32) * lam_k2.astype(jnp.float32))) + lam_init)

    hx = modulate(x, sh1, sc1)
    hc = modulate(ctx, csh1, csc1)
    px = hx @ w_in
    q_d, q_n, gl_x = split_queries_gates(px)
    k_d, v_d, k_n, v_n = split_kv(px[..., Q_COLS + GATE_W:])
    kc_d, vc_d, kc_n, vc_n = split_kv(hc @ w_in[:, Q_COLS + GATE_W:])
    cos, sin = axial_rope_tables(L, x.dtype)
    q_d = apply_axial_rope(q_d, cos, sin)
    k_d = apply_axial_rope(k_d, cos, sin)
    o_d = diff_attention(q_d, jnp.concatenate([k_d, kc_d], axis=1),
                         jnp.concatenate([v_d, vc_d], axis=1), lam, lam_init, subln_g)
    o_n = neighbourhood_attention(q_n, k_n, v_n, kc_n, vc_n, na_rpb)
    mix_x = merge_branches(o_d, o_n, gl_x, b_gate, w_branch_diff, w_branch_na, w_out)
    x_mid = layer_norm(ALPHA * x + g1 * mix_x, ln1_g, ln1_b)

    ffn_x = swiglu(modulate(x_mid, sh2, sc2), w_ffn_in, w_ffn_out)
    x_out = layer_norm(ALPHA * x_mid + g2 * ffn_x, ln2_g, ln2_b)

    if update_ctx:
        qc_d, qc_n, gl_c = split_queries_gates(hc @ w_in[:, :Q_COLS + GATE_W])
        oc_d = diff_attention(qc_d, kc_d, vc_d, lam, lam_init, subln_g)
        oc_n = context_attention(qc_n, kc_n, vc_n)
        mix_c = merge_branches(oc_d, oc_n, gl_c, b_gate, w_branch_diff, w_branch_na, w_out)
        c_mid = layer_norm(ALPHA * ctx + cg1 * mix_c, ln1_g, ln1_b)
        ffn_c = swiglu(modulate(c_mid, csh2, csc2), w_ffn_in, w_ffn_out)
        ctx = layer_norm(ALPHA * c_mid + cg2 * ffn_c, ln2_g, ln2_b)
    return x_out, ctx


def setup_inputs(seed: int = 0) -> dict:
    key = jax.random.key(seed)
    ks = jax.random.split(key, 24)
    f32 = jnp.float32
    nrm = lambda k, shape: jax.random.normal(k, shape, dtype=f32)
    D = D_MODEL
    v_scale = jnp.concatenate([
        jnp.ones((Q_COLS + GATE_W + DIFF_QK_W,), f32), jnp.full((DIFF_V_W,), BETA, f32),
        jnp.ones((NA_W,), f32), jnp.full((NA_W,), BETA, f32)])
    return {
        "x": nrm(ks[0], (BATCH, SEQ, D)),
        "c": nrm(ks[1], (BATCH, D)),
        "ctx": nrm(ks[2], (BATCH, CTX_LEN, D)),
        "c_ctx": nrm(ks[3], (D,)),
        "w_mod": nrm(ks[4], (DEPTH, D, N_MOD * D)) * (0.2 * D ** -0.5),
        "b_mod": nrm(ks[5], (DEPTH, N_MOD * D)) * 0.01,
        "w_in": nrm(ks[6], (DEPTH, D, IN_W)) * D ** -0.5 * v_scale,
        "b_gate": nrm(ks[7], (DEPTH, GATE_W)) * 0.01,
        "lam_q1": nrm(ks[8], (DEPTH, DIFF_HD)) * 0.1,
        "lam_k1": nrm(ks[9], (DEPTH, DIFF_HD)) * 0.1,
        "lam_q2": nrm(ks[10], (DEPTH, DIFF_HD)) * 0.1,
        "lam_k2": nrm(ks[11], (DEPTH, DIFF_HD)) * 0.1,
        "subln_g": 1.0 + 0.02 * nrm(ks[12], (DEPTH, 2 * DIFF_HD)),
        "na_rpb": nrm(ks[13], (DEPTH, NA_HEADS, 2 * NA_WIN_H - 1, 2 * NA_WIN_W - 1)) * 0.02,
        "w_branch_diff": nrm(ks[14], (DEPTH, DIFF_V_W, D)) * DIFF_V_W ** -0.5 * BETA,
        "w_branch_na": nrm(ks[15], (DEPTH, NA_W, D)) * NA_W ** -0.5 * BETA,
        "w_out": nrm(ks[16], (DEPTH, D, D)) * D ** -0.5 * BETA,
        "ln1_g": 1.0 + 0.02 * nrm(ks[17], (DEPTH, D)),
        "ln1_b": 0.02 * nrm(ks[18], (DEPTH, D)),
        "w_ffn_in": nrm(ks[19], (DEPTH, D, 2 * FFN_HIDDEN)) * D ** -0.5 * BETA,
        "w_ffn_out": nrm(ks[20], (DEPTH, FFN_HIDDEN, D)) * FFN_HIDDEN ** -0.5 * BETA,
        "ln2_g": 1.0 + 0.02 * nrm(ks[21], (DEPTH, D)),
        "ln2_b": 0.02 * nrm(ks[22], (DEPTH, D)),
    }


def reference(x, c, ctx, c_ctx, w_mod, b_mod, w_in, b_gate, lam_q1, lam_k1, lam_q2, lam_k2,
              subln_g, na_rpb, w_branch_diff, w_branch_na, w_out, ln1_g, ln1_b, w_ffn_in,
              w_ffn_out, ln2_g, ln2_b):
    c_silu = jax.nn.silu(c)
    c_ctx_silu = jax.nn.silu(c_ctx)
    for l in range(DEPTH):
        x, ctx = hybrid_layer(
            x, ctx, c_silu, c_ctx_silu, w_mod[l], b_mod[l], w_in[l], b_gate[l], lam_q1[l],
            lam_k1[l], lam_q2[l], lam_k2[l], subln_g[l], na_rpb[l], w_branch_diff[l],
            w_branch_na[l], w_out[l], ln1_g[l], ln1_b[l], w_ffn_in[l], w_ffn_out[l], ln2_g[l],
            ln2_b[l], layer_idx=l, update_ctx=(l < DEPTH - 1))
    return x
```

```python
import math
from contextlib import ExitStack

import numpy as np
import concourse.bass as bass
import concourse.mybir as mybir
from concourse.bass_utils import run_bass_kernel_spmd

F32 = mybir.dt.float32
BF16 = mybir.dt.bfloat16
AF = mybir.ActivationFunctionType
ALU = mybir.AluOpType
AX = mybir.AxisListType

D = 1024
L = 2048
LC = 256
LK = L + LC
NKT = LK // 128
GRID_W = 64
FFN_H = 2816
NM = FFN_H // 128
ALPHA = 2.0 ** 0.25
LAM_INIT = 0.8 - 0.6 * math.exp(0.0)
EPS = 1e-5
NEG = -30000.0
N_CORES = 8
TB = 512


class Ev:
    __slots__ = ("eng", "k", "v")

    def __init__(self, eng):
        self.eng = eng
        self.k = None
        self.v = None


class Buf:
    __slots__ = ("t", "lw", "rd", "name", "excl")

    def __init__(self, t, name="", excl=False):
        self.t = t
        self.lw = None
        self.rd = {}
        self.name = name
        self.excl = excl

    def __getitem__(self, k):
        return self.t[k]


class Ctx:
    NDS = 24
    EPOCH = 30000

    def __init__(self, nc, es):
        self.nc = nc
        self.es = es
        self.engs = {"pe": nc.tensor, "act": nc.scalar, "dve": nc.vector, "pool": nc.gpsimd, "sp": nc.sync}
        self.semh = []
        self.cur = {}
        self.cnt = {}
        self.open = {}
        self.dirty = {}
        self.last = {}
        self.waited = {e: {} for e in self.engs}
        for e in ("pe", "act", "dve", "pool"):
            self._new_sem(e)
            self.open[e] = Ev(e)
            self.dirty[e] = False
            self.last[e] = None
        self.dma_key = []
        for i in range(self.NDS):
            self.semh.append(es.enter_context(nc.semaphore("dq%d" % i)))
            self.dma_key.append(len(self.semh) - 1)
        self.dma_last = [None] * self.NDS
        self.dma_n = 0
        self.nins = 0
        self.stopped = False

    def _new_sem(self, e):
        h = self.es.enter_context(self.nc.semaphore("s_%s_%d" % (e, len(self.semh))))
        self.semh.append(h)
        self.cur[e] = len(self.semh) - 1
        self.cnt[e] = 0

    def _need(self, eng, ev, kind, dma=False):
        if ev is None:
            return
        if (not dma) and ev.eng == eng and kind != "RAW":
            return
        if ev.v is None:
            raise RuntimeError("dependency on unsignalled op (eng %s)" % ev.eng)
        w = self.waited[eng]
        if w.get(ev.k, 0) >= ev.v:
            return
        self.engs[eng].wait_ge(self.semh[ev.k], ev.v)
        w[ev.k] = ev.v

    def op(self, eng, fn, reads=(), writes=(), signal=True):
        if self.stopped:
            return None
        for b in reads:
            self._need(eng, b.lw, "RAW")
            if b.excl:
                for ev in b.rd.values():
                    self._need(eng, ev, "WAR")
        for b in writes:
            self._need(eng, b.lw, "WAW")
            for ev in b.rd.values():
                self._need(eng, ev, "WAR")
        ins = fn()
        self.nins += 1
        ev = self.open[eng]
        if signal:
            if self.cnt[eng] >= self.EPOCH:
                self._new_sem(eng)
            ins.then_inc(self.semh[self.cur[eng]], 1)
            self.cnt[eng] += 1
            ev.k = self.cur[eng]
            ev.v = self.cnt[eng]
            self.last[eng] = ev
            self.open[eng] = Ev(eng)
            self.dirty[eng] = False
        else:
            self.dirty[eng] = True
        for b in reads:
            b.rd[eng] = ev
        for b in writes:
            b.lw = ev
            b.rd = {}
        return ins

    def dma(self, q, out_ap, in_ap, reads=(), writes=(), **kw):
        if self.stopped:
            return
        for b in reads:
            self._need(q, b.lw, "RAW", dma=True)
        for b in writes:
            self._need(q, b.lw, "WAW", dma=True)
            for ev in b.rd.values():
                self._need(q, ev, "WAR", dma=True)
        n = self.dma_n
        self.dma_n += 1
        slot = n % self.NDS
        rnd = n // self.NDS
        if rnd > 0:
            self._need(q, self.dma_last[slot], "RAW", dma=True)
        k = self.dma_key[slot]
        self.engs[q].dma_start(out=out_ap, in_=in_ap, **kw).then_inc(self.semh[k], 16)
        self.nins += 1
        ev = Ev("dma")
        ev.k = k
        ev.v = 16 * (rnd + 1)
        self.dma_last[slot] = ev
        for b in reads:
            b.rd[("dma", slot)] = ev
        for b in writes:
            b.lw = ev
            b.rd = {}

    def barrier(self):
        if self.stopped:
            return
        for e in ("pe", "act", "dve", "pool"):
            assert not self.dirty[e], "unsignalled tail on %s" % e
        for e in ("pe", "act", "dve", "pool", "sp"):
            for o in ("pe", "act", "dve", "pool"):
                if o != e and self.last[o] is not None:
                    self._need(e, self.last[o], "RAW", dma=True)
            for ev in self.dma_last:
                if ev is not None:
                    self._need(e, ev, "RAW", dma=True)


def _na_tiles(j):
    if 2 <= j <= 13:
        return [(j + d, d, d + 2) for d in range(-2, 3)]
    e = {0: 0, 1: 1, 14: 2, 15: 3}[j]
    base = 0 if j < 2 else 12
    return [(base + s, base + s - j, 5 + 4 * e + s) for s in range(4)]


def _mask_tables():
    nv = 5 + 16
    m = np.full((128, nv, 128), NEG, np.float32)
    kp = np.arange(128)
    krl, kc = kp // 64, kp % 64
    qrl, qc = kp // 64, kp % 64
    cs = np.clip(qc - 8, 0, GRID_W - 16)
    col_ok = (kc[:, None] >= cs[None, :]) & (kc[:, None] < cs[None, :] + 16)
    done = set()
    for j in [5, 0, 1, 14, 15]:
        for (i, d, var) in _na_tiles(j):
            if var in done:
                continue
            done.add(var)
            kr = 2 * i + krl
            qr = 2 * j + qrl
            rs = np.clip(qr - 4, 0, 32 - 8)
            row_ok = (kr[:, None] >= rs[None, :]) & (kr[:, None] < rs[None, :] + 8)
            m[:, var, :] = np.where(row_ok & col_ok, 0.0, NEG)
    return m


def _rpb_tiles(rpb):
    kp = np.arange(128)
    krl, kc = kp // 64, kp % 64
    out = np.empty((128, 8, 7, 128), np.float32)
    ci = np.clip(kc[:, None] - kc[None, :] + 15, 0, 30)
    for d in range(-3, 4):
        ri = np.clip(2 * d + krl[:, None] - krl[None, :] + 7, 0, 14)
        out[:, :, d + 3, :] = np.transpose(rpb[:, ri, ci], (1, 0, 2))
    return np.ascontiguousarray(out.reshape(128, 4, 2, 7, 128).transpose(0, 1, 3, 2, 4))


def _rope_tables():
    t = np.arange(L)
    row = (t // GRID_W).astype(np.float32)
    col = (t % GRID_W).astype(np.float32)
    inv = (10000.0 ** (-np.arange(16, dtype=np.float32) / 16)).astype(np.float32)
    p = np.arange(128)
    axis = (p % 64) // 32
    half = (p % 32) // 16
    f = p % 16
    pos = np.where(axis[:, None] == 0, row[None, :], col[None, :]).astype(np.float32)
    ang = (pos * inv[f][:, None]).astype(np.float32)
    c = np.cos(ang).astype(np.float32)
    s = np.sin(ang).astype(np.float32) * np.where(half == 0, -1.0, 1.0).astype(np.float32)[:, None]
    perm = np.zeros((128, 128), np.float32)
    perm[p, p ^ 16] = 1.0
    return c, s.astype(np.float32), perm


def _tile_w(w, cw):
    K, N = w.shape
    return np.ascontiguousarray(w.reshape(K // 128, 128, N // cw, cw).transpose(2, 1, 0, 3))


def _fm(v):
    return np.ascontiguousarray(v.reshape(-1, 128).T)


class _Stop(Exception):
    pass


def build_program(stop=None, dbg=None):
    nc = bass.Bass("TRN2", target_bir_lowering=False)

    def din(name, shape):
        return nc.dram_tensor(name, list(shape), F32, kind="ExternalInput").ap()

    x2 = din("x2", [2, L, D])
    ctx2 = din("ctx2", [2, LC, D])
    ccT_d = din("ccT", [128, 8, 3])
    wmod_d = din("wmod_t", [12, 128, 8, 512])
    bmodT_d = din("bmodT", [128, 48])
    bmod_d = din("bmod", [6 * D])
    win_d = din("win_t", [52, 128, 8, 128])
    bgT_d = din("bgT", [128, 16])
    lamv_d = din("lamv", [4, 64])
    subln_d = din("subln", [128])
    rpbt_d = din("rpbt", [128, 4, 7, 2, 128])
    wbd_d = din("wbd_t", [8, 128, 8, 128])
    wbn_d = din("wbn_t", [8, 128, 4, 128])
    wo_d = din("wo_t", [2, 128, 8, 512])
    ln1g_d = din("ln1g", [D])
    ln1b_d = din("ln1b", [D])
    ln1gT_d = din("ln1gT", [128, 8])
    ln1bT_d = din("ln1bT", [128, 8])
    wfi_d = din("wfi_t", [44, 128, 8, 128])
    wfo_d = din("wfo_t", [2, 128, NM, 512])
    ln2g_d = din("ln2g", [D])
    ln2b_d = din("ln2b", [D])
    ident_d = din("identf", [128, 128])
    perm_d = din("permf", [128, 128])
    ropec_d = din("ropec", [128, L])
    ropes_d = din("ropes", [128, L])
    ropecq_d = din("ropecq", [128, L])
    mask_d = din("maskt", [128, 21, 128])
    y = nc.dram_tensor("y", [2, L, D], F32, kind="ExternalOutput").ap()
    dbg_d = nc.dram_tensor("dbg", [12, 128, 4096], F32, kind="ExternalOutput").ap() if dbg is not None else None

    def dscr(name, shape):
        return nc.dram_tensor(name, list(shape), BF16, kind="Internal").ap()
    wbd_b = dscr("wbd_b", [8, 128, 8, 128])
    wgt_b = dscr("wgt_b", [16, 128, 8, 128])
    wbn_b = dscr("wbn_b", [8, 128, 4, 128])
    wo_b = dscr("wo_b", [2, 128, 8, 512])
    wfi_b = dscr("wfi_b", [44, 128, 8, 128])
    wfo_b = dscr("wfo_b", [2, 128, NM, 512])
    scr = {k: Buf(None, k) for k in ("wbd", "wgt", "wbn", "wo", "wfi", "wfo")}
    conv = []
    for m_ in range(8):
        conv.append((wbd_b[m_], wbd_d[m_], scr["wbd"]))
        conv.append((wgt_b[m_], win_d[12 + m_], scr["wgt"]))
        conv.append((wbn_b[m_], wbn_d[m_], scr["wbn"]))
        conv.append((wgt_b[8 + m_], win_d[20 + m_], scr["wgt"]))
    for hf_ in range(2):
        for k0 in range(0, 8, 2):
            conv.append((wo_b[hf_, :, k0:k0 + 2, :], wo_d[hf_, :, k0:k0 + 2, :], scr["wo"]))
    for m_ in range(44):
        conv.append((wfi_b[m_], wfi_d[m_], scr["wfi"]))
    for hf_ in range(2):
        for m0 in range(0, NM, 2):
            conv.append((wfo_b[hf_, :, m0:m0 + 2, :], wfo_d[hf_, :, m0:m0 + 2, :], scr["wfo"]))

    with ExitStack() as es:
        C = Ctx(nc, es)

        def dump(slot, ap, buf, n):
            C.dma("pool", dbg_d[slot, :, 0:n], ap, reads=[buf])

        def chk(name):
            if stop == name and not C.stopped:
                C.barrier()
                C.stopped = True

        try:
            uid = [0]

            def sb(st, name, shape, dt):
                uid[0] += 1
                nm = "sb%d_%s" % (uid[0], name)
                return Buf(st.enter_context(nc.sbuf_tensor(nm, list(shape), dt)), nm)

            def act(fn, reads, writes):
                return C.op("act", fn, reads, writes)

            def dve(fn, reads, writes):
                return C.op("dve", fn, reads, writes)

            def pe(fn, reads, writes, signal=True):
                return C.op("pe", fn, reads, writes, signal)

            def pool(fn, reads, writes):
                return C.op("pool", fn, reads, writes)

            psum = es.enter_context(nc.psum_tensor("psum", [128, 4096], F32))
            bank = [Buf(psum, "bank%d" % i, excl=True) for i in range(8)]

            def pcol(b, lo=0, hi=512):
                return psum[:, b * 512 + lo:b * 512 + hi]

            ident = sb(es, "ident", [128, 128], F32)
            identb = sb(es, "identb", [128, 128], BF16)
            permb = sb(es, "permb", [128, 128], BF16)
            C.dma("sp", ident[:], ident_d, writes=[ident])
            C.dma("pool", identb[:], ident_d, writes=[identb])
            C.dma("pool", permb[:], perm_d, writes=[permb])
            lv = sb(es, "lv", [128, 4, 64], F32)
            C.dma("sp", lv[:], lamv_d.partition_broadcast(128), writes=[lv])
            lpr = sb(es, "lpr", [128, 2, 64], F32)
            lsum = sb(es, "lsum", [128, 2], F32)
            lexp = sb(es, "lexp", [128, 2], F32)
            lamneg = sb(es, "lamneg", [128, 1], F32)
            dve(lambda: nc.vector.tensor_tensor(out=lpr[:, 0, :], in0=lv[:, 0, :], in1=lv[:, 1, :], op=ALU.mult), [lv], [lpr])
            dve(lambda: nc.vector.tensor_tensor(out=lpr[:, 1, :], in0=lv[:, 2, :], in1=lv[:, 3, :], op=ALU.mult), [lv], [lpr])
            dve(lambda: nc.vector.reduce_sum(out=lsum[:], in_=lpr[:], axis=AX.X), [lpr], [lsum])
            act(lambda: nc.scalar.activation(out=lexp[:], in_=lsum[:], func=AF.Exp), [lsum], [lexp])
            dve(lambda: nc.vector.tensor_tensor(out=lamneg[:], in0=lexp[:, 1:2], in1=lexp[:, 0:1], op=ALU.subtract), [lexp], [lamneg])
            dve(lambda: nc.vector.tensor_scalar_add(out=lamneg[:], in0=lamneg[:], scalar1=-LAM_INIT), [lamneg], [lamneg])
            epsL = sb(es, "epsL", [128, 1], F32)
            dve(lambda: nc.vector.memset(epsL[:], EPS / (ALPHA * ALPHA)), [], [epsL])
            epsT = sb(es, "epsT", [128, 1], F32)
            dve(lambda: nc.vector.memset(epsT[:], EPS), [], [epsT])
            gsub = sb(es, "gsub", [128, 128], F32)
            C.dma("sp", gsub[:], subln_d.partition_broadcast(128), writes=[gsub])
            dve(lambda: nc.vector.tensor_scalar_mul(out=gsub[:], in0=gsub[:], scalar1=1.0 - LAM_INIT), [gsub], [gsub])
            bgT = sb(es, "bgT", [128, 16], F32)
            C.dma("sp", bgT[:], bgT_d, writes=[bgT])
            ln1gT = sb(es, "ln1gT", [128, 8], F32)
            ln1bT = sb(es, "ln1bT", [128, 8], F32)
            C.dma("sp", ln1gT[:], ln1gT_d, writes=[ln1gT])
            C.dma("sp", ln1bT[:], ln1bT_d, writes=[ln1bT])
            chk("setup")

            ccT = sb(es, "ccT", [128, 8, 3], F32)
            scT = sb(es, "scT", [128, 8, 3], BF16)
            scB = sb(es, "scB", [128, 2, 8, 128], BF16)
            bmT = sb(es, "bmT", [128, 48], F32)
            modT = sb(es, "modT", [128, 48, 3], F32)
            s1p = sb(es, "s1p", [128, 8, 3], F32)
            s2p = sb(es, "s2p", [128, 8, 3], F32)
            hms = sb(es, "hms", [128, 8, 2], F32)
            hmb = sb(es, "hmb", [128, 8, 2], F32)
            C.dma("sp", ccT[:], ccT_d, writes=[ccT])
            C.dma("sp", bmT[:], bmodT_d, writes=[bmT])
            act(lambda: nc.scalar.activation(out=scT[:], in_=ccT[:], func=AF.Silu), [ccT], [scT])
            for j in range(2):
                dve(lambda j=j: nc.vector.tensor_copy(out=scB[:, j, :, :], in_=scT[:, :, j:j + 1].to_broadcast([128, 8, 128])), [scT], [scB])
            with ExitStack() as ph:
                wm = [sb(ph, "wm%d" % i, [128, 8, 512], BF16) for i in range(2)]
                for blk in range(12):
                    w = wm[blk % 2]
                    C.dma("pool", w[:], wmod_d[blk], writes=[w])
                    for c4 in range(4):
                        ch = blk * 4 + c4
                        for k in range(8):
                            pe(lambda k=k, ch=ch, c4=c4, w=w: nc.tensor.matmul(pcol(0, ch * 3, ch * 3 + 3), lhsT=w[:, k, c4 * 128:(c4 + 1) * 128],
                                                                           rhs=scT[:, k, :], start=(k == 0), stop=(k == 7)),
                               [w, scT], [bank[0]], signal=(k == 7))
                dve(lambda: nc.vector.tensor_tensor(out=modT[:], in0=pcol(0, 0, 144).rearrange("p (c j) -> p c j", j=3),
                                                    in1=bmT[:].unsqueeze(2).to_broadcast([128, 48, 3]), op=ALU.add), [bank[0], bmT], [modT])
                dve(lambda: nc.vector.tensor_scalar_add(out=s1p[:], in0=modT[:, 8:16, :], scalar1=1.0), [modT], [s1p])
                dve(lambda: nc.vector.tensor_scalar_add(out=s2p[:], in0=modT[:, 32:40, :], scalar1=1.0), [modT], [s2p])
                dve(lambda: nc.vector.tensor_tensor(out=hms[:], in0=s2p[:, :, 0:2], in1=ln1gT[:].unsqueeze(2).to_broadcast([128, 8, 2]), op=ALU.mult), [s2p, ln1gT], [hms])
                dve(lambda: nc.vector.tensor_tensor(out=hmb[:], in0=s2p[:, :, 0:2], in1=ln1bT[:].unsqueeze(2).to_broadcast([128, 8, 2]), op=ALU.mult), [s2p, ln1bT], [hmb])
                dve(lambda: nc.vector.tensor_tensor(out=hmb[:], in0=hmb[:], in1=modT[:, 24:32, 0:2], op=ALU.add), [hmb, modT], [hmb])
                C.barrier()
            chk("mod")

            def build_T(src_ap, xt, psb, dst, dst_col, scale_ap_fn, bias_ap_fn, extra_reads, load=True):
                if load:
                    C.dma("sp", xt[:], src_ap, writes=[xt])
                for m in range(8):
                    pe(lambda m=m: nc.tensor.transpose(out=psum[:, psb * 512 + m * 128: psb * 512 + (m + 1) * 128], in_=xt[:, m * 128:(m + 1) * 128],
                                                       identity=ident[:]), [xt, ident], [bank[psb + m // 4]], signal=(m == 7))
                for m in range(8):
                    if m < 4:
                        act(lambda m=m: nc.scalar.activation(out=dst[:, m, dst_col:dst_col + 128], in_=psum[:, psb * 512 + m * 128: psb * 512 + (m + 1) * 128],
                                                             func=AF.Identity, scale=scale_ap_fn(m), bias=bias_ap_fn(m)),
                            [bank[psb + m // 4]] + extra_reads, [dst])
                    else:
                        dve(lambda m=m: nc.vector.tensor_scalar(out=dst[:, m, dst_col:dst_col + 128], in0=psum[:, psb * 512 + m * 128: psb * 512 + (m + 1) * 128],
                                                                scalar1=scale_ap_fn(m), scalar2=bias_ap_fn(m), op0=ALU.mult, op1=ALU.add),
                            [bank[psb + m // 4]] + extra_reads, [dst])

            for j in range(2):
                with ExitStack() as bs:
                    odT = sb(bs, "odT", [128, 8, L], BF16)
                    onT = sb(bs, "onT", [128, 4, L], BF16)
                    with ExitStack() as ph:
                        ropec = sb(ph, "ropec", [128, L], F32)
                        ropes = sb(ph, "ropes", [128, L], F32)
                        rpbt = sb(ph, "rpbt", [128, 4, 7, 2, 128], BF16)
                        maskt = sb(ph, "maskt", [128, 21, 128], BF16)
                        C.dma("sp", ropec[:], ropec_d, writes=[ropec])
                        C.dma("sp", ropes[:], ropes_d, writes=[ropes])
                        ropecq = sb(ph, "ropecq", [128, L], F32)
                        C.dma("sp", ropecq[:], ropecq_d, writes=[ropecq])
                        C.dma("pool", rpbt[:], rpbt_d, writes=[rpbt])
                        C.dma("pool", maskt[:], mask_d, writes=[maskt])
                        hxT = sb(ph, "hxT", [128, 8, LK], BF16)
                        xts = [sb(ph, "xt%d" % i, [128, D], F32) for i in range(2)]
                        for tt in range(NKT):
                            src = x2[j, tt * 128:(tt + 1) * 128, :] if tt < 16 else ctx2[j, (tt - 16) * 128:(tt - 15) * 128, :]
                            cj = j if tt < 16 else 2
                            build_T(src, xts[tt % 2], 2 * (tt % 2), hxT, tt * 128,
                                    lambda m, cj=cj: s1p[:, m, cj:cj + 1], lambda m, cj=cj: modT[:, m, cj:cj + 1], [s1p, modT])
                        chk("hx")
                        wq = sb(ph, "wq", [128, 8, 128], BF16)
                        wk = sb(ph, "wk", [128, 8, 128], BF16)
                        wv = sb(ph, "wv", [128, 8, 128], BF16)
                        qT = sb(ph, "qT", [128, 2 * L], BF16)
                        biasC = sb(ph, "biasC", [128, 5, 5, 2, 128], BF16)
                        kT = sb(ph, "kT", [128, LK], BF16)
                        Vd = sb(ph, "Vd", [128, NKT, 129], BF16)
                        Vn = sb(ph, "Vn", [128, NKT, 2, 65], BF16)
                        rawb = [sb(ph, "rawb%d" % i, [128, 512], BF16) for i in range(2)]
                        rt1 = [sb(ph, "rt1_%d" % i, [128, 512], F32) for i in range(2)]
                        rt2 = [sb(ph, "rt2_%d" % i, [128, 512], F32) for i in range(2)]
                        Pb = [sb(ph, "Pb%d" % i, [128, 1792], BF16) for i in range(2)]
                        Pd = Pb + [sb(ph, "Pb2", [128, 1024], BF16)]
                        rs = sb(ph, "rs", [128, 4, 2], F32)
                        rsl = sb(ph, "rsl", [128, 4, 1], F32)
                        oA = sb(ph, "oA", [128, 4, 128], F32)
                        oB = sb(ph, "oB", [128, 4, 128], F32)
                        ss = sb(ph, "ss", [128, 4, 1], F32)
                        rsn = sb(ph, "rsn", [128, 2, 1], F32)
                        onn = sb(ph, "onn", [128, 2, 64], F32)
                        dve(lambda: nc.vector.memset(Vd[:, :, 128:129], 1.0), [], [Vd])
                        dve(lambda: nc.vector.memset(Vn[:, :, :, 64:65], 1.0), [], [Vn])
                        chk("ms")

                        def proj_fm(wb, blk_cols, ntok, pb):
                            for k in range(8):
                                pe(lambda k=k: nc.tensor.matmul(pcol(pb, 0, ntok), lhsT=wb[:, k, :], rhs=hxT[:, k, blk_cols:blk_cols + ntok],
                                                                start=(k == 0), stop=(k == 7)), [wb, hxT], [bank[pb]], signal=(k == 7))

                        for u in range(12):
                            chk("ustart%d" % u)
                            diff = u < 8
                            cq, ck, cv = (u, 28 + u, 36 + u) if diff else (8 + (u - 8), 44 + (u - 8), 48 + (u - 8))
                            if u == 8:
                                dve(lambda: nc.vector.memset(qT[:], 0.0), [], [qT])
                            if not diff:
                                for cls in range(5):
                                    tl = _na_tiles([5, 0, 1, 14, 15][cls])
                                    nl_ = len(tl)
                                    d0, v0 = tl[0][1] + 3, tl[0][2]
                                    dve(lambda: nc.vector.tensor_tensor(out=biasC[:, cls, 0:nl_, :, :], in0=rpbt[:, u - 8, d0:d0 + nl_, :, :],
                                                                        in1=maskt[:, v0:v0 + nl_, :].unsqueeze(2).to_broadcast([128, nl_, 2, 128]), op=ALU.add),
                                        [rpbt, maskt], [biasC])
                            C.dma("pool", wq[:], win_d[cq], writes=[wq])
                            C.dma("pool", wk[:], win_d[ck], writes=[wk])
                            C.dma("pool", wv[:], win_d[cv], writes=[wv])
                            if j == 0:
                                npc = (len(conv) + 11 - u) // (12 - u) if u < 11 else len(conv)
                                for _ in range(min(npc, len(conv))):
                                    d_ap, s_ap, sbuf_ = conv.pop(0)
                                    C.dma("pool", d_ap, s_ap, writes=[sbuf_])
                            qbd = qT[:].rearrange("p (j h q) -> p j h q", h=2, q=128)
                            blocks = [("q", b_, 512) for b_ in range(4)] + [("k", b_, 512 if b_ < 4 else 256) for b_ in range(5)]
                            pend = None
                            for idx, (kind, b_, ntok) in enumerate(blocks):
                                pb = idx % 3
                                proj_fm(wq if kind == "q" else wk, b_ * 512, ntok, pb)
                                rope = diff and ntok == 512
                                if rope:
                                    rb = rawb[idx % 2]
                                    if kind == "q":
                                        act(lambda: nc.scalar.mul(out=rb[:], in_=pcol(pb), mul=0.125), [bank[pb]], [rb])
                                    else:
                                        act(lambda: nc.scalar.copy(out=rb[:], in_=pcol(pb)), [bank[pb]], [rb])
                                elif kind == "q":
                                    for hh_ in range(2):
                                        rws = slice(hh_ * 64, (hh_ + 1) * 64)
                                        act(lambda: nc.scalar.mul(out=qbd[rws, b_ * 4:(b_ + 1) * 4, hh_, :],
                                                                  in_=psum[rws, pb * 512:(pb + 1) * 512].rearrange("p (j q) -> p j q", q=128), mul=0.125), [bank[pb]], [qT])
                                else:
                                    act(lambda: nc.scalar.copy(out=kT[:, b_ * 512:b_ * 512 + ntok], in_=pcol(pb, 0, ntok)), [bank[pb]], [kT])

                                def stage2(idx=idx, kind=kind, b_=b_, pb=pb):
                                    pr = 3 + idx % 2
                                    rb = rawb[idx % 2]
                                    a1, a2 = rt1[idx % 2], rt2[idx % 2]
                                    dst = qT if kind == "q" else kT
                                    ctab = ropecq if kind == "q" else ropec
                                    tok0 = b_ * 512
                                    pe(lambda: nc.tensor.matmul(pcol(pr), lhsT=permb[:], rhs=rb[:], start=True, stop=True), [permb, rb], [bank[pr]])
                                    dve(lambda: nc.vector.tensor_tensor(out=a1[:], in0=pcol(pb), in1=ctab[:, tok0:tok0 + 512], op=ALU.mult), [bank[pb], ctab], [a1])
                                    dve(lambda: nc.vector.tensor_tensor(out=a2[:], in0=pcol(pr), in1=ropes[:, tok0:tok0 + 512], op=ALU.mult), [bank[pr], ropes], [a2])
                                    pool(lambda: nc.gpsimd.tensor_tensor(out=dst[:, tok0:tok0 + 512], in0=a1[:], in1=a2[:], op=ALU.add), [a1, a2], [dst])
                                if pend is not None:
                                    pend()
                                pend = stage2 if rope else None
                            if pend is not None:
                                pend()
                            chk("qk%d" % u)
                            for g in range(5):
                                pb = 5 + g % 2
                                nt = 4 if g < 4 else 2
                                for t4 in range(nt):
                                    tt = g * 4 + t4
                                    for k in range(8):
                                        pe(lambda k=k, tt=tt, t4=t4, pb=pb: nc.tensor.matmul(pcol(pb, t4 * 128, (t4 + 1) * 128), lhsT=hxT[:, k, tt * 128:(tt + 1) * 128],
                                                                                           rhs=wv[:, k, :], start=(k == 0), stop=(k == 7)),
                                           [hxT, wv], [bank[pb]], signal=(k == 7 and t4 == nt - 1))
                                if diff:
                                    dve(lambda g=g, nt=nt, pb=pb: nc.vector.tensor_copy(out=Vd[:, g * 4:g * 4 + nt, 0:128],
                                                                                        in_=pcol(pb, 0, nt * 128).rearrange("p (t c) -> p t c", c=128)), [bank[pb]], [Vd])
                                else:
                                    for t4 in range(nt):
                                        dve(lambda g=g, t4=t4, pb=pb: nc.vector.tensor_copy(out=Vn[:, g * 4 + t4, :, 0:64],
                                                                                            in_=pcol(pb, t4 * 128, (t4 + 1) * 128).rearrange("p (h c) -> p h c", c=64)), [bank[pb]], [Vn])
                            chk("proj%d" % u)
                            if diff:
                                h = u
                                Oall = psum[:, 2048:4096].rearrange("p (q c) -> p q c", c=512)
                                obanks = [bank[4], bank[5], bank[6], bank[7]]
                                epi_pend = [None]
                                onbufs = [rt1[0], rt1[1], rt2[0], rt2[1]]
                                for qb in range(4):
                                    def qk(kt, qb=qb):
                                        sbk = (kt % 2) * 2
                                        for hh in range(2):
                                            pe(lambda hh=hh: nc.tensor.matmul(pcol(sbk + hh), lhsT=kT[hh * 64:(hh + 1) * 64, kt * 128:(kt + 1) * 128],
                                                                              rhs=qT[hh * 64:(hh + 1) * 64, qb * 512:(qb + 1) * 512], start=True, stop=True),
                                               [kT, qT], [bank[sbk + hh]], signal=(hh == 1))
                                        act(lambda: nc.scalar.activation(out=Pd[kt % 3][:, 0:1024], in_=psum[:, sbk * 512:(sbk + 2) * 512], func=AF.Exp),
                                            [bank[sbk], bank[sbk + 1]], [Pd[kt % 3]])

                                    def av(kt):
                                        P = Pd[kt % 3]
                                        for qi in range(4):
                                            for hh in range(2):
                                                pe(lambda qi=qi, hh=hh: nc.tensor.matmul(psum[:, (4 + qi) * 512 + hh * 129:(4 + qi) * 512 + hh * 129 + 129],
                                                                                         lhsT=P[:, hh * 512 + qi * 128: hh * 512 + (qi + 1) * 128], rhs=Vd[:, kt, :],
                                                                                         start=(kt == 0 and hh == 0), stop=(kt == NKT - 1)),
                                                   [P, Vd], [bank[4 + qi]], signal=(qi == 3 and hh == 1))
                                    qk(0)
                                    qk(1)
                                    for kt in range(NKT):
                                        if kt + 2 < NKT:
                                            qk(kt + 2)
                                        av(kt)
                                        if kt == 3 and epi_pend[0] is not None:
                                            epi_pend[0]()
                                            epi_pend[0] = None
                                    dve(lambda: nc.vector.reciprocal(out=rs[:], in_=Oall[:, :, 128:258:129]), obanks, [rs])
                                    dve(lambda: nc.vector.tensor_scalar(out=rsl[:], in0=rs[:, :, 1:2], scalar1=lamneg[:, 0:1], scalar2=None, op0=ALU.mult), [rs, lamneg], [rsl])
                                    dve(lambda: nc.vector.tensor_tensor(out=oA[:], in0=Oall[:, :, 0:128], in1=rs[:, :, 0:1].to_broadcast([128, 4, 128]), op=ALU.mult), obanks + [rs], [oA])
                                    dve(lambda: nc.vector.tensor_tensor(out=oB[:], in0=Oall[:, :, 129:257], in1=rsl[:].to_broadcast([128, 4, 128]), op=ALU.mult), obanks + [rsl], [oB])
                                    dve(lambda: nc.vector.tensor_tensor(out=oA[:], in0=oA[:], in1=oB[:], op=ALU.add), [oA, oB], [oA])
                                    dve(lambda: nc.vector.tensor_tensor(out=oB[:], in0=oA[:], in1=oA[:], op=ALU.mult), [oA], [oB])
                                    dve(lambda: nc.vector.reduce_sum(out=ss[:, :, 0], in_=oB[:], axis=AX.X), [oB], [ss])

                                    def epi_tail(qb=qb):
                                        onb = onbufs[qb]
                                        act(lambda: nc.scalar.activation(out=ss[:], in_=ss[:], func=AF.Ln, scale=1.0 / 128, bias=epsT[:, 0:1]), [ss, epsT], [ss])
                                        act(lambda: nc.scalar.activation(out=ss[:], in_=ss[:], func=AF.Exp, scale=-0.5), [ss], [ss])
                                        dve(lambda: nc.vector.tensor_tensor(out=oB[:], in0=oA[:], in1=ss[:].to_broadcast([128, 4, 128]), op=ALU.mult), [oA, ss], [oB])
                                        dve(lambda: nc.vector.tensor_tensor(out=onb[:].rearrange("p (q e) -> p q e", e=128), in0=oB[:],
                                                                            in1=gsub[:].unsqueeze(1).to_broadcast([128, 4, 128]), op=ALU.mult), [oB, gsub], [onb])
                                    epi_pend[0] = epi_tail
                                epi_pend[0]()
                                epi_pend[0] = None
                                for qb in range(4):
                                    onb = onbufs[qb]
                                    for qi in range(4):
                                        pe(lambda: nc.tensor.transpose(out=pcol(4 + qb, qi * 128, (qi + 1) * 128), in_=onb[:, qi * 128:(qi + 1) * 128], identity=ident[:]),
                                           [onb, ident], [bank[4 + qb]], signal=(qi == 3))
                                    act(lambda: nc.scalar.copy(out=odT[:, h, qb * 512:(qb + 1) * 512], in_=pcol(4 + qb)), [bank[4 + qb]], [odT])
                            else:
                                c = u - 8
                                qbd = qT[:].rearrange("p (j h q) -> p j h q", h=2, q=128)
                                def na_qk(jq):
                                    tiles = _na_tiles(jq)
                                    nl = len(tiles)
                                    ntot = nl + 2
                                    cls = {0: 1, 1: 2, 14: 3, 15: 4}.get(jq, 0)
                                    P = Pb[jq % 2]
                                    started = set()
                                    for s0 in range(0, nl, 2):
                                        ns = min(2, nl - s0)
                                        bk = s0 // 2
                                        pe(lambda: nc.tensor.matmul(psum[:, s0 * 256:(s0 + ns) * 256], lhsT=identb[:],
                                                                    rhs=biasC[:, cls, s0:s0 + ns, :, :].rearrange("p s h q -> p (s h q)"), start=True, stop=False),
                                           [identb, biasC], [bank[bk]], signal=False)
                                        started.add(bk)
                                    klist = [t[0] for t in tiles] + [16, 17]
                                    for s_, i in enumerate(klist):
                                        bk = s_ // 2
                                        st_ = bk not in started
                                        started.add(bk)
                                        pe(lambda: nc.tensor.matmul(psum[:, s_ * 256:(s_ + 1) * 256], lhsT=kT[:, i * 128:(i + 1) * 128],
                                                                    rhs=qbd[:, jq, :, :].rearrange("p h q -> p (h q)"), start=st_, stop=True),
                                           [kT, qT], [bank[bk]], signal=(s_ == 3 or s_ == ntot - 1))
                                    act(lambda: nc.scalar.activation(out=P[:, 0:1024], in_=psum[:, 0:1024], func=AF.Exp), [bank[0], bank[1]], [P])
                                    act(lambda: nc.scalar.activation(out=P[:, 1024:ntot * 256], in_=psum[:, 1024:ntot * 256], func=AF.Exp), [bank[2], bank[3]], [P])

                                def na_av(jq):
                                    tiles = _na_tiles(jq)
                                    ntot = len(tiles) + 2
                                    klist = [t[0] for t in tiles] + [16, 17]
                                    ob = 4 + jq % 2
                                    P = Pb[jq % 2]
                                    Pv = P[:].rearrange("p (s h q) -> p s h q", h=2, q=128)
                                    for hh in range(2):
                                        for s_, i in enumerate(klist):
                                            pe(lambda: nc.tensor.matmul(pcol(ob, hh * 65, hh * 65 + 65), lhsT=Pv[:, s_, hh, :], rhs=Vn[:, i, hh, :],
                                                                        start=(s_ == 0), stop=(s_ == ntot - 1)), [P, Vn], [bank[ob]], signal=(s_ == ntot - 1))
                                    dve(lambda: nc.vector.reciprocal(out=rsn[:], in_=pcol(ob, 0, 130).rearrange("p (h c) -> p h c", c=65)[:, :, 64:65]), [bank[ob]], [rsn])
                                    dve(lambda: nc.vector.tensor_tensor(out=onn[:], in0=pcol(ob, 0, 130).rearrange("p (h c) -> p h c", c=65)[:, :, 0:64],
                                                                        in1=rsn[:].to_broadcast([128, 2, 64]), op=ALU.mult), [bank[ob], rsn], [onn])
                                    tb_ = 6 + jq % 2
                                    pe(lambda: nc.tensor.transpose(out=pcol(tb_, 0, 128), in_=onn[:].rearrange("p h c -> p (h c)"), identity=ident[:]), [onn, ident], [bank[tb_]])
                                    act(lambda: nc.scalar.copy(out=onT[:, c, jq * 128:(jq + 1) * 128], in_=pcol(tb_, 0, 128)), [bank[tb_]], [onT])

                                na_qk(0)
                                for jq in range(16):
                                    if jq + 1 < 16:
                                        na_qk(jq + 1)
                                    na_av(jq)
                        C.barrier()

                    with ExitStack() as ph:
                        g1b = sb(ph, "g1b", [128, D], F32)
                        g2b = sb(ph, "g2b", [128, D], F32)
                        l1g = sb(ph, "l1g", [128, D], F32)
                        l1b = sb(ph, "l1b", [128, D], F32)
                        l2g = sb(ph, "l2g", [128, D], F32)
                        l2b = sb(ph, "l2b", [128, D], F32)
                        C.dma("sp", l1g[:], ln1g_d.partition_broadcast(128), writes=[l1g])
                        C.dma("sp", l1b[:], ln1b_d.partition_broadcast(128), writes=[l1b])
                        C.dma("sp", l2g[:], ln2g_d.partition_broadcast(128), writes=[l2g])
                        C.dma("sp", l2b[:], ln2b_d.partition_broadcast(128), writes=[l2b])
                        wbig = [sb(ph, "wbig%d" % i, [128, NM, 512], BF16) for i in range(2)]
                        wc = [sb(ph, "wc%d" % i, [128, 8, 128], BF16) for i in range(6)]
                        xs = [sb(ph, "xs%d" % i, [128, D], BF16) for i in range(2)]
                        wci = [0]

                        def next_wc():
                            b = wc[wci[0] % 6]
                            wci[0] += 1
                            return b
                        tmpa = sb(ph, "tmpa", [128, D], F32)
                        tmpn = sb(ph, "tmpn", [128, D], F32)
                        for gi, gb in ((2, g1b), (5, g2b)):
                            for hf in range(2):
                                w = wbig[hf]
                                C.dma("pool", w[:, 0:8, :], wmod_d[gi * 2 + hf], writes=[w])
                                C.dma("sp", tmpa[:, 0:512], bmod_d[gi * D + hf * 512: gi * D + (hf + 1) * 512].partition_broadcast(128), writes=[tmpa])
                                for k in range(8):
                                    pe(lambda k=k, w=w: nc.tensor.matmul(pcol(0), lhsT=scB[:, j, k, :], rhs=w[:, k, :], start=(k == 0), stop=(k == 7)),
                                       [scB, w], [bank[0]], signal=(k == 7))
                                dve(lambda gb=gb, hf=hf: nc.vector.tensor_tensor(out=gb[:, hf * 512:(hf + 1) * 512], in0=pcol(0), in1=tmpa[:, 0:512], op=ALU.add),
                                    [bank[0], tmpa], [gb])
                            dve(lambda gb=gb: nc.vector.tensor_scalar_mul(out=gb[:], in0=gb[:], scalar1=1.0 / ALPHA), [gb], [gb])
                        chk("gb")
                        if dbg is not None and j == 0:
                            dump(0, g1b[:], g1b, 1024)
                            dump(1, g2b[:], g2b, 1024)
                        xm = [sb(ph, "xm%d" % i, [128, D], F32) for i in range(4)]
                        hT8 = sb(ph, "hT8", [128, 8, TB], BF16)
                        yT = sb(ph, "yT", [128, 8, TB], BF16)
                        hT = sb(ph, "hT", [128, NM, TB], BF16)
                        sg = [sb(ph, "sg%d" % i, [128, TB], F32) for i in range(2)]
                        st = sb(ph, "st", [128, 2, 6], F32)
                        mv = sb(ph, "mv", [128, 2], F32)
                        rstd = sb(ph, "rstd", [128, 1], F32)
                        nb = sb(ph, "nb", [128, 1], F32)

                        def ln_stats(r):
                            for hf in range(2):
                                dve(lambda hf=hf: nc.vector.bn_stats(out=st[:, hf, :], in_=r[:, hf * 512:(hf + 1) * 512]), [r], [st])
                            dve(lambda: nc.vector.bn_aggr(out=mv[:], in_=st[:].rearrange("p a b -> p (a b)")), [st], [mv])
                            act(lambda: nc.scalar.activation(out=rstd[:], in_=mv[:, 1:2], func=AF.Ln, bias=epsL[:, 0:1]), [mv, epsL], [rstd])
                            act(lambda: nc.scalar.activation(out=rstd[:], in_=rstd[:], func=AF.Exp, scale=-0.5), [rstd], [rstd])

                        def build_hx(tbn):
                            for ti in range(4):
                                xsb = xs[ti % 2]
                                C.dma("pool", xsb[:], x2[j, tbn * TB + ti * 128: tbn * TB + (ti + 1) * 128, :], writes=[xsb])
                                pv = psum[:, ti * 512:(ti + 1) * 512].bitcast(BF16)
                                for m in range(8):
                                    pe(lambda m=m: nc.tensor.transpose(out=pv[:, m * 128:(m + 1) * 128], in_=xsb[:, m * 128:(m + 1) * 128], identity=identb[:]),
                                       [xsb, identb], [bank[ti]], signal=(m == 7))
                                for m in range(8):
                                    act(lambda m=m: nc.scalar.activation(out=hT8[:, m, ti * 128:(ti + 1) * 128], in_=pv[:, m * 128:(m + 1) * 128], func=AF.Identity,
                                                                         scale=s1p[:, m, j:j + 1], bias=modT[:, m, j:j + 1]), [bank[ti], s1p, modT], [hT8])

                        for tb in range(L // TB):
                            t0 = tb * TB
                            chk("blk%d_%d" % (j, tb))
                            if tb == 0:
                                build_hx(0)
                            chk("pa")
                            for m in range(8):
                                if m == 4:
                                    for ti in range(4):
                                        C.dma("sp", xm[ti][:], x2[j, t0 + ti * 128: t0 + (ti + 1) * 128, :], writes=[xm[ti]])
                                if m == 3:
                                    for hf in range(2):
                                        C.dma("sp", wbig[hf][:, 0:8, :], wo_b[hf], reads=[scr["wo"]], writes=[wbig[hf]])
                                wb_bd, wb_gd, wb_bn, wb_gn = next_wc(), next_wc(), next_wc(), next_wc()
                                C.dma("sp", wb_bd[:], wbd_b[m], reads=[scr["wbd"]], writes=[wb_bd])
                                C.dma("sp", wb_gd[:], wgt_b[m], reads=[scr["wgt"]], writes=[wb_gd])
                                C.dma("sp", wb_bn[:, 0:4, :], wbn_b[m], reads=[scr["wbn"]], writes=[wb_bn])
                                C.dma("sp", wb_gn[:], wgt_b[8 + m], reads=[scr["wgt"]], writes=[wb_gn])
                                b0 = 4 * (m % 2)
                                sga, sgb, Y = sg[0], sg[1], (tmpa if m % 2 == 0 else tmpn)
                                for k in range(8):
                                    pe(lambda k=k: nc.tensor.matmul(pcol(b0), lhsT=wb_bd[:, k, :], rhs=odT[:, k, t0:t0 + TB], start=(k == 0), stop=(k == 7)),
                                       [wb_bd, odT], [bank[b0]], signal=(k == 7))
                                for k in range(8):
                                    pe(lambda k=k: nc.tensor.matmul(pcol(b0 + 1), lhsT=wb_gd[:, k, :], rhs=hT8[:, k, :], start=(k == 0), stop=(k == 7)),
                                       [wb_gd, hT8], [bank[b0 + 1]], signal=(k == 7))
                                for k in range(4):
                                    pe(lambda k=k: nc.tensor.matmul(pcol(b0 + 2), lhsT=wb_bn[:, k, :], rhs=onT[:, k, t0:t0 + TB], start=(k == 0), stop=(k == 3)),
                                       [wb_bn, onT], [bank[b0 + 2]], signal=(k == 3))
                                for k in range(8):
                                    pe(lambda k=k: nc.tensor.matmul(pcol(b0 + 3), lhsT=wb_gn[:, k, :], rhs=hT8[:, k, :], start=(k == 0), stop=(k == 7)),
                                       [wb_gn, hT8], [bank[b0 + 3]], signal=(k == 7))
                                act(lambda: nc.scalar.activation(out=sga[:], in_=pcol(b0 + 1), func=AF.Sigmoid, bias=bgT[:, m:m + 1]), [bank[b0 + 1], bgT], [sga])
                                act(lambda: nc.scalar.activation(out=sgb[:], in_=pcol(b0 + 3), func=AF.Sigmoid, bias=bgT[:, 8 + m:9 + m]), [bank[b0 + 3], bgT], [sgb])
                                dve(lambda: nc.vector.tensor_tensor(out=Y[:, 0:512], in0=sga[:], in1=pcol(b0), op=ALU.mult), [sga, bank[b0]], [Y])
                                dve(lambda: nc.vector.tensor_tensor(out=Y[:, 512:1024], in0=sgb[:], in1=pcol(b0 + 2), op=ALU.mult), [sgb, bank[b0 + 2]], [Y])
                                dve(lambda: nc.vector.tensor_tensor(out=yT[:, m, :], in0=Y[:, 0:512], in1=Y[:, 512:1024], op=ALU.add), [Y], [yT])
                            chk("pb")
                            mixb = [0, 2, 4, 0]

                            def mix(ti):
                                for hf in range(2):
                                    pb = mixb[ti] + hf
                                    for k in range(8):
                                        pe(lambda k=k: nc.tensor.matmul(pcol(pb), lhsT=yT[:, k, ti * 128:(ti + 1) * 128], rhs=wbig[hf][:, k, :],
                                                                        start=(k == 0), stop=(k == 7)), [yT, wbig[hf]], [bank[pb]], signal=(k == 7))

                            def chain(ti):
                                pb0 = mixb[ti]
                                r = xm[ti]
                                dve(lambda: nc.vector.tensor_tensor(out=tmpa[:], in0=psum[:, pb0 * 512:(pb0 + 2) * 512], in1=g1b[:], op=ALU.mult),
                                    [bank[pb0], bank[pb0 + 1], g1b], [tmpa])
                                isd = dbg is not None and j == 0 and tb == dbg // 4 and ti == dbg % 4
                                if isd:
                                    dump(2, tmpa[:], tmpa, 1024)
                                    for m_ in range(8):
                                        C.dma("pool", dbg_d[3, :, m_ * 128:(m_ + 1) * 128], odT[:, m_, t0 + ti * 128:t0 + (ti + 1) * 128], reads=[odT])
                                    for m_ in range(4):
                                        C.dma("pool", dbg_d[4, :, m_ * 128:(m_ + 1) * 128], onT[:, m_, t0 + ti * 128:t0 + (ti + 1) * 128], reads=[onT])
                                    for m_ in range(8):
                                        C.dma("pool", dbg_d[5, :, m_ * 128:(m_ + 1) * 128], yT[:, m_, ti * 128:(ti + 1) * 128], reads=[yT])
                                dve(lambda: nc.vector.tensor_tensor(out=r[:], in0=r[:], in1=tmpa[:], op=ALU.add), [r, tmpa], [r])
                                ln_stats(r)
                                dve(lambda: nc.vector.tensor_scalar(out=nb[:], in0=mv[:, 0:1], scalar1=rstd[:, 0:1], scalar2=-1.0, op0=ALU.mult, op1=ALU.mult), [mv, rstd], [nb])
                                act(lambda: nc.scalar.activation(out=tmpn[:], in_=r[:], func=AF.Identity, scale=rstd[:, 0:1], bias=nb[:, 0:1]), [r, rstd, nb], [tmpn])
                                pool(lambda: nc.gpsimd.tensor_tensor(out=r[:], in0=tmpn[:], in1=l1g[:], op=ALU.mult), [tmpn, l1g], [r])
                                pool(lambda: nc.gpsimd.tensor_tensor(out=r[:], in0=r[:], in1=l1b[:], op=ALU.add), [r, l1b], [r])
                                if isd:
                                    dump(6, r[:], r, 1024)
                                build_T(None, tmpn, 6, hT8, ti * 128, lambda m: hms[:, m, j:j + 1], lambda m: hmb[:, m, j:j + 1], [hms, hmb], load=False)

                            mix(0)
                            mix(1)
                            mix(2)
                            chain(0)
                            mix(3)
                            chain(1)
                            chain(2)
                            chain(3)
                            chk("pc")
                            for m in range(NM):
                                if m == 5:
                                    for hf in range(2):
                                        C.dma("sp", wbig[hf][:], wfo_b[hf], reads=[scr["wfo"]], writes=[wbig[hf]])
                                wg, wu = next_wc(), next_wc()
                                C.dma("sp", wg[:], wfi_b[m], reads=[scr["wfi"]], writes=[wg])
                                C.dma("sp", wu[:], wfi_b[NM + m], reads=[scr["wfi"]], writes=[wu])
                                pg, pu = 2 * (m % 2), 2 * (m % 2) + 1
                                for k in range(8):
                                    pe(lambda k=k: nc.tensor.matmul(pcol(pg), lhsT=wg[:, k, :], rhs=hT8[:, k, :], start=(k == 0), stop=(k == 7)), [wg, hT8], [bank[pg]], signal=(k == 7))
                                for k in range(8):
                                    pe(lambda k=k: nc.tensor.matmul(pcol(pu), lhsT=wu[:, k, :], rhs=hT8[:, k, :], start=(k == 0), stop=(k == 7)), [wu, hT8], [bank[pu]], signal=(k == 7))
                                s_ = sg[m % 2]
                                act(lambda: nc.scalar.activation(out=s_[:], in_=pcol(pg), func=AF.Silu), [bank[pg]], [s_])
                                dve(lambda m=m: nc.vector.tensor_tensor(out=hT[:, m, :], in0=s_[:], in1=pcol(pu), op=ALU.mult), [s_, bank[pu]], [hT])
                            chk("pd")
                            if tb + 1 < L // TB:
                                build_hx(tb + 1)
                            for hf in range(2):
                                cs = slice(hf * 512, (hf + 1) * 512)
                                for ti in range(4):
                                    pb = 4 + ti
                                    for m in range(NM):
                                        pe(lambda m=m: nc.tensor.matmul(pcol(pb), lhsT=hT[:, m, ti * 128:(ti + 1) * 128], rhs=wbig[hf][:, m, :],
                                                                        start=(m == 0), stop=(m == NM - 1)), [hT, wbig[hf]], [bank[pb]], signal=(m == NM - 1))
                                    r = xm[ti]
                                    dve(lambda: nc.vector.tensor_tensor(out=tmpa[:, cs], in0=pcol(pb), in1=g2b[:, cs], op=ALU.mult), [bank[pb], g2b], [tmpa])
                                    dve(lambda: nc.vector.tensor_tensor(out=r[:, cs], in0=r[:, cs], in1=tmpa[:, cs], op=ALU.add), [r, tmpa], [r])
                                    if hf == 1:
                                        r = xm[ti]
                                        if dbg is not None and j == 0 and tb == dbg // 4 and ti == dbg % 4:
                                            dump(7, r[:], r, 1024)
                                            for m_ in range(NM):
                                                C.dma("pool", dbg_d[8, :, m_ * 128:(m_ + 1) * 128], hT[:, m_, ti * 128:(ti + 1) * 128], reads=[hT])
                                            for m_ in range(8):
                                                C.dma("pool", dbg_d[9, :, m_ * 128:(m_ + 1) * 128], hT8[:, m_, ti * 128:(ti + 1) * 128], reads=[hT8])
                                        ln_stats(r)
                                        dve(lambda: nc.vector.tensor_scalar(out=nb[:], in0=mv[:, 0:1], scalar1=rstd[:, 0:1], scalar2=-1.0, op0=ALU.mult, op1=ALU.mult), [mv, rstd], [nb])
                                        act(lambda: nc.scalar.activation(out=tmpn[:], in_=r[:], func=AF.Identity, scale=rstd[:, 0:1], bias=nb[:, 0:1]), [r, rstd, nb], [tmpn])
                                        pool(lambda: nc.gpsimd.tensor_tensor(out=tmpn[:], in0=tmpn[:], in1=l2g[:], op=ALU.mult), [tmpn, l2g], [tmpn])
                                        pool(lambda: nc.gpsimd.tensor_tensor(out=r[:], in0=tmpn[:], in1=l2b[:], op=ALU.add), [tmpn, l2b], [r])
                                        C.dma("pool", y[j, t0 + ti * 128: t0 + (ti + 1) * 128, :], r[:], reads=[r])
                        C.barrier()
        except _Stop:
            pass
        C.barrier()
        print("program built: %d instructions, %d dmas, %d sems" % (C.nins, C.dma_n, len(C.semh)))
    return nc


_PROG = None


def kernel(**inputs):
    global _PROG
    f = lambda a: np.ascontiguousarray(np.asarray(a, dtype=np.float32))
    x = f(inputs["x"])
    c = f(inputs["c"])
    ctx = f(inputs["ctx"])
    c_ctx = f(inputs["c_ctx"])
    ropec, ropes, perm = _rope_tables()
    shared = {
        "wmod_t": _tile_w(f(inputs["w_mod"])[0], 512),
        "bmodT": _fm(f(inputs["b_mod"])[0]),
        "bmod": f(inputs["b_mod"])[0],
        "win_t": _tile_w(f(inputs["w_in"])[0], 128),
        "bgT": _fm(f(inputs["b_gate"])[0]),
        "lamv": np.ascontiguousarray(np.stack([f(inputs["lam_q1"])[0], f(inputs["lam_k1"])[0], f(inputs["lam_q2"])[0], f(inputs["lam_k2"])[0]])),
        "subln": f(inputs["subln_g"])[0],
        "rpbt": _rpb_tiles(f(inputs["na_rpb"])[0]),
        "wbd_t": _tile_w(f(inputs["w_branch_diff"])[0], 128),
        "wbn_t": _tile_w(f(inputs["w_branch_na"])[0], 128),
        "wo_t": _tile_w(f(inputs["w_out"])[0], 512),
        "ln1g": f(inputs["ln1_g"])[0],
        "ln1b": f(inputs["ln1_b"])[0],
        "ln1gT": _fm(f(inputs["ln1_g"])[0]),
        "ln1bT": _fm(f(inputs["ln1_b"])[0]),
        "wfi_t": _tile_w(f(inputs["w_ffn_in"])[0], 128),
        "wfo_t": _tile_w(f(inputs["w_ffn_out"])[0], 512),
        "ln2g": f(inputs["ln2_g"])[0],
        "ln2b": f(inputs["ln2_b"])[0],
        "identf": np.eye(128, dtype=np.float32),
        "permf": perm,
        "ropec": ropec,
        "ropes": ropes,
        "ropecq": np.ascontiguousarray(ropec * np.float32(0.125)),
        "maskt": _mask_tables(),
    }
    in_maps = []
    for core in range(N_CORES):
        b0 = 2 * core
        cc = np.stack([c[b0], c[b0 + 1], c_ctx])
        ccT = np.ascontiguousarray(cc.reshape(3, 8, 128).transpose(2, 1, 0))
        m = dict(shared)
        m["x2"] = np.ascontiguousarray(x[b0:b0 + 2])
        m["ctx2"] = np.ascontiguousarray(ctx[b0:b0 + 2])
        m["ccT"] = ccT
        in_maps.append(m)
    if _PROG is None:
        _PROG = build_program()
    res = run_bass_kernel_spmd(_PROG, in_maps, core_ids=list(range(N_CORES)))
    out = np.concatenate([np.asarray(r["y"], dtype=np.float32) for r in res.results], axis=0)
    return out
```

```python
import math
from contextlib import ExitStack

import numpy as np
import concourse.bass as bass
import concourse.mybir as mybir
from concourse.bass_utils import run_bass_kernel_spmd

F32 = mybir.dt.float32
BF16 = mybir.dt.bfloat16
AF = mybir.ActivationFunctionType
ALU = mybir.AluOpType
AX = mybir.AxisListType

D = 1024
L = 2048
LC = 256
LK = L + LC
NKT = LK // 128
GRID_W = 64
FFN_H = 2816
NM = FFN_H // 128
ALPHA = 2.0 ** 0.25
LAM_INIT = 0.8 - 0.6 * math.exp(0.0)
EPS = 1e-5
NEG = -30000.0
N_CORES = 8
TB = 512


class Ev:
    __slots__ = ("eng", "k", "v")

    def __init__(self, eng):
        self.eng = eng
        self.k = None
        self.v = None


class Buf:
    __slots__ = ("t", "lw", "rd", "name", "excl")

    def __init__(self, t, name="", excl=False):
        self.t = t
        self.lw = None
        self.rd = {}
        self.name = name
        self.excl = excl

    def __getitem__(self, k):
        return self.t[k]


class Ctx:
    NDS = 24
    EPOCH = 30000

    def __init__(self, nc, es):
        self.nc = nc
        self.es = es
        self.engs = {"pe": nc.tensor, "act": nc.scalar, "dve": nc.vector, "pool": nc.gpsimd, "sp": nc.sync}
        self.semh = []
        self.cur = {}
        self.cnt = {}
        self.open = {}
        self.dirty = {}
        self.last = {}
        self.waited = {e: {} for e in self.engs}
        for e in ("pe", "act", "dve", "pool"):
            self._new_sem(e)
            self.open[e] = Ev(e)
            self.dirty[e] = False
            self.last[e] = None
        self.dma_key = []
        for i in range(self.NDS):
            self.semh.append(es.enter_context(nc.semaphore("dq%d" % i)))
            self.dma_key.append(len(self.semh) - 1)
        self.dma_last = [None] * self.NDS
        self.dma_n = 0
        self.nins = 0
        self.stopped = False

    def _new_sem(self, e):
        h = self.es.enter_context(self.nc.semaphore("s_%s_%d" % (e, len(self.semh))))
        self.semh.append(h)
        self.cur[e] = len(self.semh) - 1
        self.cnt[e] = 0

    def _need(self, eng, ev, kind, dma=False):
        if ev is None:
            return
        if (not dma) and ev.eng == eng and kind != "RAW":
            return
        if ev.v is None:
            raise RuntimeError("dependency on unsignalled op (eng %s)" % ev.eng)
        w = self.waited[eng]
        if w.get(ev.k, 0) >= ev.v:
            return
        self.engs[eng].wait_ge(self.semh[ev.k], ev.v)
        w[ev.k] = ev.v

    def op(self, eng, fn, reads=(), writes=(), signal=True):
        if self.stopped:
            return None
        for b in reads:
            self._need(eng, b.lw, "RAW")
            if b.excl:
                for ev in b.rd.values():
                    self._need(eng, ev, "WAR")
        for b in writes:
            self._need(eng, b.lw, "WAW")
            for ev in b.rd.values():
                self._need(eng, ev, "WAR")
        ins = fn()
        self.nins += 1
        ev = self.open[eng]
        if signal:
            if self.cnt[eng] >= self.EPOCH:
                self._new_sem(eng)
            ins.then_inc(self.semh[self.cur[eng]], 1)
            self.cnt[eng] += 1
            ev.k = self.cur[eng]
            ev.v = self.cnt[eng]
            self.last[eng] = ev
            self.open[eng] = Ev(eng)
            self.dirty[eng] = False
        else:
            self.dirty[eng] = True
        for b in reads:
            b.rd[eng] = ev
        for b in writes:
            b.lw = ev
            b.rd = {}
        return ins

    def dma(self, q, out_ap, in_ap, reads=(), writes=(), **kw):
        if self.stopped:
            return
        for b in reads:
            self._need(q, b.lw, "RAW", dma=True)
        for b in writes:
            self._need(q, b.lw, "WAW", dma=True)
            for ev in b.rd.values():
                self._need(q, ev, "WAR", dma=True)
        n = self.dma_n
        self.dma_n += 1
        slot = n % self.NDS
        rnd = n // self.NDS
        if rnd > 0:
            self._need(q, self.dma_last[slot], "RAW", dma=True)
        k = self.dma_key[slot]
        self.engs[q].dma_start(out=out_ap, in_=in_ap, **kw).then_inc(self.semh[k], 16)
        self.nins += 1
        ev = Ev("dma")
        ev.k = k
        ev.v = 16 * (rnd + 1)
        self.dma_last[slot] = ev
        for b in reads:
            b.rd[("dma", slot)] = ev
        for b in writes:
            b.lw = ev
            b.rd = {}

    def barrier(self):
        if self.stopped:
            return
        for e in ("pe", "act", "dve", "pool"):
            assert not self.dirty[e], "unsignalled tail on %s" % e
        for e in ("pe", "act", "dve", "pool", "sp"):
            for o in ("pe", "act", "dve", "pool"):
                if o != e and self.last[o] is not None:
                    self._need(e, self.last[o], "RAW", dma=True)
            for ev in self.dma_last:
                if ev is not None:
                    self._need(e, ev, "RAW", dma=True)


def _na_tiles(j):
    if 2 <= j <= 13:
        return [(j + d, d, d + 2) for d in range(-2, 3)]
    e = {0: 0, 1: 1, 14: 2, 15: 3}[j]
    base = 0 if j < 2 else 12
    return [(base + s, base + s - j, 5 + 4 * e + s) for s in range(4)]


def _mask_tables():
    nv = 5 + 16
    m = np.full((128, nv, 128), NEG, np.float32)
    kp = np.arange(128)
    krl, kc = kp // 64, kp % 64
    qrl, qc = kp // 64, kp % 64
    cs = np.clip(qc - 8, 0, GRID_W - 16)
    col_ok = (kc[:, None] >= cs[None, :]) & (kc[:, None] < cs[None, :] + 16)
    done = set()
    for j in [5, 0, 1, 14, 15]:
        for (i, d, var) in _na_tiles(j):
            if var in done:
                continue
            done.add(var)
            kr = 2 * i + krl
            qr = 2 * j + qrl
            rs = np.clip(qr - 4, 0, 32 - 8)
            row_ok = (kr[:, None] >= rs[None, :]) & (kr[:, None] < rs[None, :] + 8)
            m[:, var, :] = np.where(row_ok & col_ok, 0.0, NEG)
    return m


def _rpb_tiles(rpb):
    kp = np.arange(128)
    krl, kc = kp // 64, kp % 64
    out = np.empty((128, 8, 7, 128), np.float32)
    ci = np.clip(kc[:, None] - kc[None, :] + 15, 0, 30)
    for d in range(-3, 4):
        ri = np.clip(2 * d + krl[:, None] - krl[None, :] + 7, 0, 14)
        out[:, :, d + 3, :] = np.transpose(rpb[:, ri, ci], (1, 0, 2))
    return np.ascontiguousarray(out.reshape(128, 4, 2, 7, 128).transpose(0, 1, 3, 2, 4))


def _rope_tables():
    t = np.arange(L)
    row = (t // GRID_W).astype(np.float32)
    col = (t % GRID_W).astype(np.float32)
    inv = (10000.0 ** (-np.arange(16, dtype=np.float32) / 16)).astype(np.float32)
    p = np.arange(128)
    axis = (p % 64) // 32
    half = (p % 32) // 16
    f = p % 16
    pos = np.where(axis[:, None] == 0, row[None, :], col[None, :]).astype(np.float32)
    ang = (pos * inv[f][:, None]).astype(np.float32)
    c = np.cos(ang).astype(np.float32)
    s = np.sin(ang).astype(np.float32) * np.where(half == 0, -1.0, 1.0).astype(np.float32)[:, None]
    perm = np.zeros((128, 128), np.float32)
    perm[p, p ^ 16] = 1.0
    return c, s.astype(np.float32), perm


def _tile_w(w, cw):
    K, N = w.shape
    return np.ascontiguousarray(w.reshape(K // 128, 128, N // cw, cw).transpose(2, 1, 0, 3))


def _fm(v):
    return np.ascontiguousarray(v.reshape(-1, 128).T)


class _Stop(Exception):
    pass


def build_program(stop=None, dbg=None):
    nc = bass.Bass("TRN2", target_bir_lowering=False)

    def din(name, shape):
        return nc.dram_tensor(name, list(shape), F32, kind="ExternalInput").ap()

    x2 = din("x2", [2, L, D])
    ctx2 = din("ctx2", [2, LC, D])
    ccT_d = din("ccT", [128, 8, 3])
    wmod_d = din("wmod_t", [12, 128, 8, 512])
    bmodT_d = din("bmodT", [128, 48])
    bmod_d = din("bmod", [6 * D])
    win_d = din("win_t", [52, 128, 8, 128])
    bgT_d = din("bgT", [128, 16])
    lamv_d = din("lamv", [4, 64])
    subln_d = din("subln", [128])
    rpbt_d = din("rpbt", [128, 4, 7, 2, 128])
    wbd_d = din("wbd_t", [8, 128, 8, 128])
    wbn_d = din("wbn_t", [8, 128, 4, 128])
    wo_d = din("wo_t", [2, 128, 8, 512])
    ln1g_d = din("ln1g", [D])
    ln1b_d = din("ln1b", [D])
    ln1gT_d = din("ln1gT", [128, 8])
    ln1bT_d = din("ln1bT", [128, 8])
    wfi_d = din("wfi_t", [44, 128, 8, 128])
    wfo_d = din("wfo_t", [2, 128, NM, 512])
    ln2g_d = din("ln2g", [D])
    ln2b_d = din("ln2b", [D])
    ident_d = din("identf", [128, 128])
    perm_d = din("permf", [128, 128])
    ropec_d = din("ropec", [128, L])
    ropes_d = din("ropes", [128, L])
    ropecq_d = din("ropecq", [128, L])
    mask_d = din("maskt", [128, 21, 128])
    y = nc.dram_tensor("y", [2, L, D], F32, kind="ExternalOutput").ap()
    dbg_d = nc.dram_tensor("dbg", [12, 128, 4096], F32, kind="ExternalOutput").ap() if dbg is not None else None

    def dscr(name, shape):
        return nc.dram_tensor(name, list(shape), BF16, kind="Internal").ap()
    wbd_b = dscr("wbd_b", [8, 128, 8, 128])
    wgt_b = dscr("wgt_b", [16, 128, 8, 128])
    wbn_b = dscr("wbn_b", [8, 128, 4, 128])
    wo_b = dscr("wo_b", [2, 128, 8, 512])
    wfi_b = dscr("wfi_b", [44, 128, 8, 128])
    wfo_b = dscr("wfo_b", [2, 128, NM, 512])
    scr = {k: Buf(None, k) for k in ("wbd", "wgt", "wbn", "wo", "wfi", "wfo")}
    conv = []
    for m_ in range(8):
        conv.append((wbd_b[m_], wbd_d[m_], scr["wbd"]))
        conv.append((wgt_b[m_], win_d[12 + m_], scr["wgt"]))
        conv.append((wbn_b[m_], wbn_d[m_], scr["wbn"]))
        conv.append((wgt_b[8 + m_], win_d[20 + m_], scr["wgt"]))
    for hf_ in range(2):
        for k0 in range(0, 8, 2):
            conv.append((wo_b[hf_, :, k0:k0 + 2, :], wo_d[hf_, :, k0:k0 + 2, :], scr["wo"]))
    for m_ in range(44):
        conv.append((wfi_b[m_], wfi_d[m_], scr["wfi"]))
    for hf_ in range(2):
        for m0 in range(0, NM, 2):
            conv.append((wfo_b[hf_, :, m0:m0 + 2, :], wfo_d[hf_, :, m0:m0 + 2, :], scr["wfo"]))

    with ExitStack() as es:
        C = Ctx(nc, es)

        def dump(slot, ap, buf, n):
            C.dma("pool", dbg_d[slot, :, 0:n], ap, reads=[buf])

        def chk(name):
            if stop == name and not C.stopped:
                C.barrier()
                C.stopped = True

        try:
            uid = [0]

            def sb(st, name, shape, dt):
                uid[0] += 1
                nm = "sb%d_%s" % (uid[0], name)
                return Buf(st.enter_context(nc.sbuf_tensor(nm, list(shape), dt)), nm)

            def act(fn, reads, writes):
                return C.op("act", fn, reads, writes)

            def dve(fn, reads, writes):
                return C.op("dve", fn, reads, writes)

            def pe(fn, reads, writes, signal=True):
                return C.op("pe", fn, reads, writes, signal)

            def pool(fn, reads, writes):
                return C.op("pool", fn, reads, writes)

            psum = es.enter_context(nc.psum_tensor("psum", [128, 4096], F32))
            bank = [Buf(psum, "bank%d" % i, excl=True) for i in range(8)]

            def pcol(b, lo=0, hi=512):
                return psum[:, b * 512 + lo:b * 512 + hi]

            ident = sb(es, "ident", [128, 128], F32)
            identb = sb(es, "identb", [128, 128], BF16)
            permb = sb(es, "permb", [128, 128], BF16)
            C.dma("sp", ident[:], ident_d, writes=[ident])
            C.dma("pool", identb[:], ident_d, writes=[identb])
            C.dma("pool", permb[:], perm_d, writes=[permb])
            lv = sb(es, "lv", [128, 4, 64], F32)
            C.dma("sp", lv[:], lamv_d.partition_broadcast(128), writes=[lv])
            lpr = sb(es, "lpr", [128, 2, 64], F32)
            lsum = sb(es, "lsum", [128, 2], F32)
            lexp = sb(es, "lexp", [128, 2], F32)
            lamneg = sb(es, "lamneg", [128, 1], F32)
            dve(lambda: nc.vector.tensor_tensor(out=lpr[:, 0, :], in0=lv[:, 0, :], in1=lv[:, 1, :], op=ALU.mult), [lv], [lpr])
            dve(lambda: nc.vector.tensor_tensor(out=lpr[:, 1, :], in0=lv[:, 2, :], in1=lv[:, 3, :], op=ALU.mult), [lv], [lpr])
            dve(lambda: nc.vector.reduce_sum(out=lsum[:], in_=lpr[:], axis=AX.X), [lpr], [lsum])
            act(lambda: nc.scalar.activation(out=lexp[:], in_=lsum[:], func=AF.Exp), [lsum], [lexp])
            dve(lambda: nc.vector.tensor_tensor(out=lamneg[:], in0=lexp[:, 1:2], in1=lexp[:, 0:1], op=ALU.subtract), [lexp], [lamneg])
            dve(lambda: nc.vector.tensor_scalar_add(out=lamneg[:], in0=lamneg[:], scalar1=-LAM_INIT), [lamneg], [lamneg])
            epsL = sb(es, "epsL", [128, 1], F32)
            dve(lambda: nc.vector.memset(epsL[:], EPS / (ALPHA * ALPHA)), [], [epsL])
            epsT = sb(es, "epsT", [128, 1], F32)
            dve(lambda: nc.vector.memset(epsT[:], EPS), [], [epsT])
            gsub = sb(es, "gsub", [128, 128], F32)
            C.dma("sp", gsub[:], subln_d.partition_broadcast(128), writes=[gsub])
            dve(lambda: nc.vector.tensor_scalar_mul(out=gsub[:], in0=gsub[:], scalar1=1.0 - LAM_INIT), [gsub], [gsub])
            bgT = sb(es, "bgT", [128, 16], F32)
            C.dma("sp", bgT[:], bgT_d, writes=[bgT])
            ln1gT = sb(es, "ln1gT", [128, 8], F32)
            ln1bT = sb(es, "ln1bT", [128, 8], F32)
            C.dma("sp", ln1gT[:], ln1gT_d, writes=[ln1gT])
            C.dma("sp", ln1bT[:], ln1bT_d, writes=[ln1bT])
            chk("setup")

            ccT = sb(es, "ccT", [128, 8, 3], F32)
            scT = sb(es, "scT", [128, 8, 3], BF16)
            scB = sb(es, "scB", [128, 2, 8, 128], BF16)
            bmT = sb(es, "bmT", [128, 48], F32)
            modT = sb(es, "modT", [128, 48, 3], F32)
            s1p = sb(es, "s1p", [128, 8, 3], F32)
            s2p = sb(es, "s2p", [128, 8, 3], F32)
            hms = sb(es, "hms", [128, 8, 2], F32)
            hmb = sb(es, "hmb", [128, 8, 2], F32)
            C.dma("sp", ccT[:], ccT_d, writes=[ccT])
            C.dma("sp", bmT[:], bmodT_d, writes=[bmT])
            act(lambda: nc.scalar.activation(out=scT[:], in_=ccT[:], func=AF.Silu), [ccT], [scT])
            for j in range(2):
                dve(lambda j=j: nc.vector.tensor_copy(out=scB[:, j, :, :], in_=scT[:, :, j:j + 1].to_broadcast([128, 8, 128])), [scT], [scB])
            with ExitStack() as ph:
                wm = [sb(ph, "wm%d" % i, [128, 8, 512], BF16) for i in range(2)]
                for blk in range(12):
                    w = wm[blk % 2]
                    C.dma("pool", w[:], wmod_d[blk], writes=[w])
                    for c4 in range(4):
                        ch = blk * 4 + c4
                        for k in range(8):
                            pe(lambda k=k, ch=ch, c4=c4, w=w: nc.tensor.matmul(pcol(0, ch * 3, ch * 3 + 3), lhsT=w[:, k, c4 * 128:(c4 + 1) * 128],
                                                                           rhs=scT[:, k, :], start=(k == 0), stop=(k == 7)),
                               [w, scT], [bank[0]], signal=(k == 7))
                dve(lambda: nc.vector.tensor_tensor(out=modT[:], in0=pcol(0, 0, 144).rearrange("p (c j) -> p c j", j=3),
                                                    in1=bmT[:].unsqueeze(2).to_broadcast([128, 48, 3]), op=ALU.add), [bank[0], bmT], [modT])
                dve(lambda: nc.vector.tensor_scalar_add(out=s1p[:], in0=modT[:, 8:16, :], scalar1=1.0), [modT], [s1p])
                dve(lambda: nc.vector.tensor_scalar_add(out=s2p[:], in0=modT[:, 32:40, :], scalar1=1.0), [modT], [s2p])
                dve(lambda: nc.vector.tensor_tensor(out=hms[:], in0=s2p[:, :, 0:2], in1=ln1gT[:].unsqueeze(2).to_broadcast([128, 8, 2]), op=ALU.mult), [s2p, ln1gT], [hms])
                dve(lambda: nc.vector.tensor_tensor(out=hmb[:], in0=s2p[:, :, 0:2], in1=ln1bT[:].unsqueeze(2).to_broadcast([128, 8, 2]), op=ALU.mult), [s2p, ln1bT], [hmb])
                dve(lambda: nc.vector.tensor_tensor(out=hmb[:], in0=hmb[:], in1=modT[:, 24:32, 0:2], op=ALU.add), [hmb, modT], [hmb])
                C.barrier()
            chk("mod")

            def build_T(src_ap, xt, psb, dst, dst_col, scale_ap_fn, bias_ap_fn, extra_reads, load=True):
                if load:
                    C.dma("sp", xt[:], src_ap, writes=[xt])
                for m in range(8):
                    pe(lambda m=m: nc.tensor.transpose(out=psum[:, psb * 512 + m * 128: psb * 512 + (m + 1) * 128], in_=xt[:, m * 128:(m + 1) * 128],
                                                       identity=ident[:]), [xt, ident], [bank[psb + m // 4]], signal=(m == 7))
                for m in range(8):
                    if m < 4 or not load:
                        act(lambda m=m: nc.scalar.activation(out=dst[:, m, dst_col:dst_col + 128], in_=psum[:, psb * 512 + m * 128: psb * 512 + (m + 1) * 128],
                                                             func=AF.Identity, scale=scale_ap_fn(m), bias=bias_ap_fn(m)),
                            [bank[psb + m // 4]] + extra_reads, [dst])
                    else:
                        dve(lambda m=m: nc.vector.tensor_scalar(out=dst[:, m, dst_col:dst_col + 128], in0=psum[:, psb * 512 + m * 128: psb * 512 + (m + 1) * 128],
                                                                scalar1=scale_ap_fn(m), scalar2=bias_ap_fn(m), op0=ALU.mult, op1=ALU.add),
                            [bank[psb + m // 4]] + extra_reads, [dst])

            for j in range(2):
                with ExitStack() as bs:
                    odT = sb(bs, "odT", [128, 8, L], BF16)
                    onT = sb(bs, "onT", [128, 4, L], BF16)
                    with ExitStack() as ph:
                        ropec = sb(ph, "ropec", [128, L], F32)
                        ropes = sb(ph, "ropes", [128, L], F32)
                        rpbt = sb(ph, "rpbt", [128, 4, 7, 2, 128], BF16)
                        maskt = sb(ph, "maskt", [128, 21, 128], BF16)
                        C.dma("sp", ropec[:], ropec_d, writes=[ropec])
                        C.dma("sp", ropes[:], ropes_d, writes=[ropes])
                        ropecq = sb(ph, "ropecq", [128, L], F32)
                        C.dma("sp", ropecq[:], ropecq_d, writes=[ropecq])
                        C.dma("pool", rpbt[:], rpbt_d, writes=[rpbt])
                        C.dma("pool", maskt[:], mask_d, writes=[maskt])
                        hxT = sb(ph, "hxT", [128, 8, LK], BF16)
                        xts = [sb(ph, "xt%d" % i, [128, D], F32) for i in range(2)]
                        for tt in range(NKT):
                            src = x2[j, tt * 128:(tt + 1) * 128, :] if tt < 16 else ctx2[j, (tt - 16) * 128:(tt - 15) * 128, :]
                            cj = j if tt < 16 else 2
                            build_T(src, xts[tt % 2], 2 * (tt % 2), hxT, tt * 128,
                                    lambda m, cj=cj: s1p[:, m, cj:cj + 1], lambda m, cj=cj: modT[:, m, cj:cj + 1], [s1p, modT])
                        chk("hx")
                        wq = sb(ph, "wq", [128, 8, 128], BF16)
                        wk = sb(ph, "wk", [128, 8, 128], BF16)
                        wv = sb(ph, "wv", [128, 8, 128], BF16)
                        qT = sb(ph, "qT", [128, 2 * L], BF16)
                        biasC = sb(ph, "biasC", [128, 5, 5, 2, 128], BF16)
                        kT = sb(ph, "kT", [128, LK], BF16)
                        Vd = sb(ph, "Vd", [128, NKT, 129], BF16)
                        Vn = sb(ph, "Vn", [128, NKT, 2, 65], BF16)
                        rawb = [sb(ph, "rawb%d" % i, [128, 512], BF16) for i in range(2)]
                        rt1 = [sb(ph, "rt1_%d" % i, [128, 512], F32) for i in range(2)]
                        rt2 = [sb(ph, "rt2_%d" % i, [128, 512], F32) for i in range(2)]
                        Pb = [sb(ph, "Pb%d" % i, [128, 1792], BF16) for i in range(2)]
                        Pd = Pb + [sb(ph, "Pb2", [128, 1024], BF16)]
                        rs = sb(ph, "rs", [128, 4, 2], F32)
                        rsl = sb(ph, "rsl", [128, 4, 1], F32)
                        oA = sb(ph, "oA", [128, 4, 128], F32)
                        oB = sb(ph, "oB", [128, 4, 128], F32)
                        ss = sb(ph, "ss", [128, 4, 1], F32)
                        rsn = sb(ph, "rsn", [128, 2, 1], F32)
                        onn = sb(ph, "onn", [128, 2, 64], F32)
                        dve(lambda: nc.vector.memset(Vd[:, :, 128:129], 1.0), [], [Vd])
                        dve(lambda: nc.vector.memset(Vn[:, :, :, 64:65], 1.0), [], [Vn])
                        chk("ms")

                        def proj_fm(wb, blk_cols, ntok, pb):
                            for k in range(8):
                                pe(lambda k=k: nc.tensor.matmul(pcol(pb, 0, ntok), lhsT=wb[:, k, :], rhs=hxT[:, k, blk_cols:blk_cols + ntok],
                                                                start=(k == 0), stop=(k == 7)), [wb, hxT], [bank[pb]], signal=(k == 7))

                        for u in range(12):
                            chk("ustart%d" % u)
                            diff = u < 8
                            cq, ck, cv = (u, 28 + u, 36 + u) if diff else (8 + (u - 8), 44 + (u - 8), 48 + (u - 8))
                            if u == 8:
                                dve(lambda: nc.vector.memset(qT[:], 0.0), [], [qT])
                            if not diff:
                                for cls in range(5):
                                    tl = _na_tiles([5, 0, 1, 14, 15][cls])
                                    nl_ = len(tl)
                                    d0, v0 = tl[0][1] + 3, tl[0][2]
                                    dve(lambda: nc.vector.tensor_tensor(out=biasC[:, cls, 0:nl_, :, :], in0=rpbt[:, u - 8, d0:d0 + nl_, :, :],
                                                                        in1=maskt[:, v0:v0 + nl_, :].unsqueeze(2).to_broadcast([128, nl_, 2, 128]), op=ALU.add),
                                        [rpbt, maskt], [biasC])
                            C.dma("pool", wq[:], win_d[cq], writes=[wq])
                            C.dma("pool", wk[:], win_d[ck], writes=[wk])
                            C.dma("pool", wv[:], win_d[cv], writes=[wv])
                            if j == 0:
                                npc = (len(conv) + 11 - u) // (12 - u) if u < 11 else len(conv)
                                for _ in range(min(npc, len(conv))):
                                    d_ap, s_ap, sbuf_ = conv.pop(0)
                                    C.dma("pool", d_ap, s_ap, writes=[sbuf_])
                            qbd = qT[:].rearrange("p (j h q) -> p j h q", h=2, q=128)
                            blocks = [("q", b_, 512) for b_ in range(4)] + [("k", b_, 512 if b_ < 4 else 256) for b_ in range(5)]
                            pend = None
                            for idx, (kind, b_, ntok) in enumerate(blocks):
                                pb = idx % 3
                                proj_fm(wq if kind == "q" else wk, b_ * 512, ntok, pb)
                                rope = diff and ntok == 512
                                if rope:
                                    rb = rawb[idx % 2]
                                    if kind == "q":
                                        act(lambda: nc.scalar.mul(out=rb[:], in_=pcol(pb), mul=0.125), [bank[pb]], [rb])
                                    else:
                                        act(lambda: nc.scalar.copy(out=rb[:], in_=pcol(pb)), [bank[pb]], [rb])
                                elif kind == "q":
                                    for hh_ in range(2):
                                        rws = slice(hh_ * 64, (hh_ + 1) * 64)
                                        act(lambda: nc.scalar.mul(out=qbd[rws, b_ * 4:(b_ + 1) * 4, hh_, :],
                                                                  in_=psum[rws, pb * 512:(pb + 1) * 512].rearrange("p (j q) -> p j q", q=128), mul=0.125), [bank[pb]], [qT])
                                else:
                                    act(lambda: nc.scalar.copy(out=kT[:, b_ * 512:b_ * 512 + ntok], in_=pcol(pb, 0, ntok)), [bank[pb]], [kT])

                                def stage2(idx=idx, kind=kind, b_=b_, pb=pb):
                                    pr = 3 + idx % 2
                                    rb = rawb[idx % 2]
                                    a1, a2 = rt1[idx % 2], rt2[idx % 2]
                                    dst = qT if kind == "q" else kT
                                    ctab = ropecq if kind == "q" else ropec
                                    tok0 = b_ * 512
                                    pe(lambda: nc.tensor.matmul(pcol(pr), lhsT=permb[:], rhs=rb[:], start=True, stop=True), [permb, rb], [bank[pr]])
                                    dve(lambda: nc.vector.tensor_tensor(out=a1[:], in0=pcol(pb), in1=ctab[:, tok0:tok0 + 512], op=ALU.mult), [bank[pb], ctab], [a1])
                                    dve(lambda: nc.vector.tensor_tensor(out=a2[:], in0=pcol(pr), in1=ropes[:, tok0:tok0 + 512], op=ALU.mult), [bank[pr], ropes], [a2])
                                    pool(lambda: nc.gpsimd.tensor_tensor(out=dst[:, tok0:tok0 + 512], in0=a1[:], in1=a2[:], op=ALU.add), [a1, a2], [dst])
                                if pend is not None:
                                    pend()
                                pend = stage2 if rope else None
                            if pend is not None:
                                pend()
                            chk("qk%d" % u)
                            for g in range(5):
                                pb = 5 + g % 2
                                nt = 4 if g < 4 else 2
                                for t4 in range(nt):
                                    tt = g * 4 + t4
                                    for k in range(8):
                                        pe(lambda k=k, tt=tt, t4=t4, pb=pb: nc.tensor.matmul(pcol(pb, t4 * 128, (t4 + 1) * 128), lhsT=hxT[:, k, tt * 128:(tt + 1) * 128],
                                                                                           rhs=wv[:, k, :], start=(k == 0), stop=(k == 7)),
                                           [hxT, wv], [bank[pb]], signal=(k == 7 and t4 == nt - 1))
                                if diff:
                                    dve(lambda g=g, nt=nt, pb=pb: nc.vector.tensor_copy(out=Vd[:, g * 4:g * 4 + nt, 0:128],
                                                                                        in_=pcol(pb, 0, nt * 128).rearrange("p (t c) -> p t c", c=128)), [bank[pb]], [Vd])
                                else:
                                    for t4 in range(nt):
                                        dve(lambda g=g, t4=t4, pb=pb: nc.vector.tensor_copy(out=Vn[:, g * 4 + t4, :, 0:64],
                                                                                            in_=pcol(pb, t4 * 128, (t4 + 1) * 128).rearrange("p (h c) -> p h c", c=64)), [bank[pb]], [Vn])
                            chk("proj%d" % u)
                            if diff:
                                h = u
                                Oall = psum[:, 2048:4096].rearrange("p (q c) -> p q c", c=512)
                                obanks = [bank[4], bank[5], bank[6], bank[7]]
                                epi_pend = [None]
                                onbufs = [rt1[0], rt1[1], rt2[0], rt2[1]]
                                for qb in range(4):
                                    def qk(kt, qb=qb):
                                        sbk = (kt % 2) * 2
                                        for hh in range(2):
                                            pe(lambda hh=hh: nc.tensor.matmul(pcol(sbk + hh), lhsT=kT[hh * 64:(hh + 1) * 64, kt * 128:(kt + 1) * 128],
                                                                              rhs=qT[hh * 64:(hh + 1) * 64, qb * 512:(qb + 1) * 512], start=True, stop=True),
                                               [kT, qT], [bank[sbk + hh]], signal=(hh == 1))
                                        act(lambda: nc.scalar.activation(out=Pd[kt % 3][:, 0:1024], in_=psum[:, sbk * 512:(sbk + 2) * 512], func=AF.Exp),
                                            [bank[sbk], bank[sbk + 1]], [Pd[kt % 3]])

                                    def av(kt):
                                        P = Pd[kt % 3]
                                        for qi in range(4):
                                            for hh in range(2):
                                                pe(lambda qi=qi, hh=hh: nc.tensor.matmul(psum[:, (4 + qi) * 512 + hh * 129:(4 + qi) * 512 + hh * 129 + 129],
                                                                                         lhsT=P[:, hh * 512 + qi * 128: hh * 512 + (qi + 1) * 128], rhs=Vd[:, kt, :],
                                                                                         start=(kt == 0 and hh == 0), stop=(kt == NKT - 1)),
                                                   [P, Vd], [bank[4 + qi]], signal=(qi == 3 and hh == 1))
                                    qk(0)
                                    qk(1)
                                    for kt in range(NKT):
                                        if kt + 2 < NKT:
                                            qk(kt + 2)
                                        av(kt)
                                        if kt == 3 and epi_pend[0] is not None:
                                            epi_pend[0]()
                                            epi_pend[0] = None
                                    dve(lambda: nc.vector.reciprocal(out=rs[:], in_=Oall[:, :, 128:258:129]), obanks, [rs])
                                    dve(lambda: nc.vector.tensor_scalar(out=rsl[:], in0=rs[:, :, 1:2], scalar1=lamneg[:, 0:1], scalar2=None, op0=ALU.mult), [rs, lamneg], [rsl])
                                    dve(lambda: nc.vector.tensor_tensor(out=oA[:], in0=Oall[:, :, 0:128], in1=rs[:, :, 0:1].to_broadcast([128, 4, 128]), op=ALU.mult), obanks + [rs], [oA])
                                    dve(lambda: nc.vector.tensor_tensor(out=oB[:], in0=Oall[:, :, 129:257], in1=rsl[:].to_broadcast([128, 4, 128]), op=ALU.mult), obanks + [rsl], [oB])
                                    dve(lambda: nc.vector.tensor_tensor(out=oA[:], in0=oA[:], in1=oB[:], op=ALU.add), [oA, oB], [oA])
                                    dve(lambda: nc.vector.tensor_tensor(out=oB[:], in0=oA[:], in1=oA[:], op=ALU.mult), [oA], [oB])
                                    dve(lambda: nc.vector.reduce_sum(out=ss[:, :, 0], in_=oB[:], axis=AX.X), [oB], [ss])

                                    def epi_tail(qb=qb):
                                        onb = onbufs[qb]
                                        act(lambda: nc.scalar.activation(out=ss[:], in_=ss[:], func=AF.Ln, scale=1.0 / 128, bias=epsT[:, 0:1]), [ss, epsT], [ss])
                                        act(lambda: nc.scalar.activation(out=ss[:], in_=ss[:], func=AF.Exp, scale=-0.5), [ss], [ss])
                                        dve(lambda: nc.vector.tensor_tensor(out=oB[:], in0=oA[:], in1=ss[:].to_broadcast([128, 4, 128]), op=ALU.mult), [oA, ss], [oB])
                                        dve(lambda: nc.vector.tensor_tensor(out=onb[:].rearrange("p (q e) -> p q e", e=128), in0=oB[:],
                                                                            in1=gsub[:].unsqueeze(1).to_broadcast([128, 4, 128]), op=ALU.mult), [oB, gsub], [onb])
                                    epi_pend[0] = epi_tail
                                epi_pend[0]()
                                epi_pend[0] = None
                                for qb in range(4):
                                    onb = onbufs[qb]
                                    for qi in range(4):
                                        pe(lambda: nc.tensor.transpose(out=pcol(4 + qb, qi * 128, (qi + 1) * 128), in_=onb[:, qi * 128:(qi + 1) * 128], identity=ident[:]),
                                           [onb, ident], [bank[4 + qb]], signal=(qi == 3))
                                    act(lambda: nc.scalar.copy(out=odT[:, h, qb * 512:(qb + 1) * 512], in_=pcol(4 + qb)), [bank[4 + qb]], [odT])
                            else:
                                c = u - 8
                                qbd = qT[:].rearrange("p (j h q) -> p j h q", h=2, q=128)
                                def na_qk(jq):
                                    tiles = _na_tiles(jq)
                                    nl = len(tiles)
                                    ntot = nl + 2
                                    cls = {0: 1, 1: 2, 14: 3, 15: 4}.get(jq, 0)
                                    P = Pb[jq % 2]
                                    started = set()
                                    for s0 in range(0, nl, 2):
                                        ns = min(2, nl - s0)
                                        bk = s0 // 2
                                        pe(lambda: nc.tensor.matmul(psum[:, s0 * 256:(s0 + ns) * 256], lhsT=identb[:],
                                                                    rhs=biasC[:, cls, s0:s0 + ns, :, :].rearrange("p s h q -> p (s h q)"), start=True, stop=False),
                                           [identb, biasC], [bank[bk]], signal=False)
                                        started.add(bk)
                                    klist = [t[0] for t in tiles] + [16, 17]
                                    for s_, i in enumerate(klist):
                                        bk = s_ // 2
                                        st_ = bk not in started
                                        started.add(bk)
                                        pe(lambda: nc.tensor.matmul(psum[:, s_ * 256:(s_ + 1) * 256], lhsT=kT[:, i * 128:(i + 1) * 128],
                                                                    rhs=qbd[:, jq, :, :].rearrange("p h q -> p (h q)"), start=st_, stop=True),
                                           [kT, qT], [bank[bk]], signal=(s_ == 3 or s_ == ntot - 1))
                                    act(lambda: nc.scalar.activation(out=P[:, 0:1024], in_=psum[:, 0:1024], func=AF.Exp), [bank[0], bank[1]], [P])
                                    act(lambda: nc.scalar.activation(out=P[:, 1024:ntot * 256], in_=psum[:, 1024:ntot * 256], func=AF.Exp), [bank[2], bank[3]], [P])

                                def na_av(jq):
                                    tiles = _na_tiles(jq)
                                    ntot = len(tiles) + 2
                                    klist = [t[0] for t in tiles] + [16, 17]
                                    ob = 4 + jq % 2
                                    P = Pb[jq % 2]
                                    Pv = P[:].rearrange("p (s h q) -> p s h q", h=2, q=128)
                                    for hh in range(2):
                                        for s_, i in enumerate(klist):
                                            pe(lambda: nc.tensor.matmul(pcol(ob, hh * 65, hh * 65 + 65), lhsT=Pv[:, s_, hh, :], rhs=Vn[:, i, hh, :],
                                                                        start=(s_ == 0), stop=(s_ == ntot - 1)), [P, Vn], [bank[ob]], signal=(s_ == ntot - 1))
                                    dve(lambda: nc.vector.reciprocal(out=rsn[:], in_=pcol(ob, 0, 130).rearrange("p (h c) -> p h c", c=65)[:, :, 64:65]), [bank[ob]], [rsn])
                                    dve(lambda: nc.vector.tensor_tensor(out=onn[:], in0=pcol(ob, 0, 130).rearrange("p (h c) -> p h c", c=65)[:, :, 0:64],
                                                                        in1=rsn[:].to_broadcast([128, 2, 64]), op=ALU.mult), [bank[ob], rsn], [onn])
                                    tb_ = 6 + jq % 2
                                    pe(lambda: nc.tensor.transpose(out=pcol(tb_, 0, 128), in_=onn[:].rearrange("p h c -> p (h c)"), identity=ident[:]), [onn, ident], [bank[tb_]])
                                    act(lambda: nc.scalar.copy(out=onT[:, c, jq * 128:(jq + 1) * 128], in_=pcol(tb_, 0, 128)), [bank[tb_]], [onT])

                                na_qk(0)
                                for jq in range(16):
                                    if jq + 1 < 16:
                                        na_qk(jq + 1)
                                    na_av(jq)
                        C.barrier()

                    with ExitStack() as ph:
                        g1b = sb(ph, "g1b", [128, D], F32)
                        g2b = sb(ph, "g2b", [128, D], F32)
                        l1g = sb(ph, "l1g", [128, D], F32)
                        l1b = sb(ph, "l1b", [128, D], F32)
                        l2g = sb(ph, "l2g", [128, D], F32)
                        l2b = sb(ph, "l2b", [128, D], F32)
                        C.dma("sp", l1g[:], ln1g_d.partition_broadcast(128), writes=[l1g])
                        C.dma("sp", l1b[:], ln1b_d.partition_broadcast(128), writes=[l1b])
                        C.dma("sp", l2g[:], ln2g_d.partition_broadcast(128), writes=[l2g])
                        C.dma("sp", l2b[:], ln2b_d.partition_broadcast(128), writes=[l2b])
                        wbig = [sb(ph, "wbig%d" % i, [128, NM, 512], BF16) for i in range(2)]
                        wc = [sb(ph, "wc%d" % i, [128, 8, 128], BF16) for i in range(6)]
                        xs = [sb(ph, "xs%d" % i, [128, D], BF16) for i in range(2)]
                        wci = [0]

                        def next_wc():
                            b = wc[wci[0] % 6]
                            wci[0] += 1
                            return b
                        tmpa = sb(ph, "tmpa", [128, D], F32)
                        tmpn = sb(ph, "tmpn", [128, D], F32)
                        for gi, gb in ((2, g1b), (5, g2b)):
                            for hf in range(2):
                                w = wbig[hf]
                                C.dma("pool", w[:, 0:8, :], wmod_d[gi * 2 + hf], writes=[w])
                                C.dma("sp", tmpa[:, 0:512], bmod_d[gi * D + hf * 512: gi * D + (hf + 1) * 512].partition_broadcast(128), writes=[tmpa])
                                for k in range(8):
                                    pe(lambda k=k, w=w: nc.tensor.matmul(pcol(0), lhsT=scB[:, j, k, :], rhs=w[:, k, :], start=(k == 0), stop=(k == 7)),
                                       [scB, w], [bank[0]], signal=(k == 7))
                                dve(lambda gb=gb, hf=hf: nc.vector.tensor_tensor(out=gb[:, hf * 512:(hf + 1) * 512], in0=pcol(0), in1=tmpa[:, 0:512], op=ALU.add),
                                    [bank[0], tmpa], [gb])
                            dve(lambda gb=gb: nc.vector.tensor_scalar_mul(out=gb[:], in0=gb[:], scalar1=1.0 / ALPHA), [gb], [gb])
                        chk("gb")
                        if dbg is not None and j == 0:
                            dump(0, g1b[:], g1b, 1024)
                            dump(1, g2b[:], g2b, 1024)
                        xm = [sb(ph, "xm%d" % i, [128, D], F32) for i in range(4)]
                        hT8 = sb(ph, "hT8", [128, 8, TB], BF16)
                        yT = sb(ph, "yT", [128, 8, TB], BF16)
                        hT = sb(ph, "hT", [128, NM, TB], BF16)
                        sg = [sb(ph, "sg%d" % i, [128, TB], F32) for i in range(2)]
                        st = sb(ph, "st", [128, 2, 6], F32)
                        mv = sb(ph, "mv", [128, 2], F32)
                        rstd = sb(ph, "rstd", [128, 1], F32)
                        nb = sb(ph, "nb", [128, 1], F32)

                        def ln_stats(r):
                            for hf in range(2):
                                dve(lambda hf=hf: nc.vector.bn_stats(out=st[:, hf, :], in_=r[:, hf * 512:(hf + 1) * 512]), [r], [st])
                            dve(lambda: nc.vector.bn_aggr(out=mv[:], in_=st[:].rearrange("p a b -> p (a b)")), [st], [mv])
                            act(lambda: nc.scalar.activation(out=rstd[:], in_=mv[:, 1:2], func=AF.Ln, bias=epsL[:, 0:1]), [mv, epsL], [rstd])
                            act(lambda: nc.scalar.activation(out=rstd[:], in_=rstd[:], func=AF.Exp, scale=-0.5), [rstd], [rstd])

                        def build_hx(tbn):
                            for ti in range(4):
                                xsb = xs[ti % 2]
                                C.dma("pool", xsb[:], x2[j, tbn * TB + ti * 128: tbn * TB + (ti + 1) * 128, :], writes=[xsb])
                                pv = psum[:, ti * 512:(ti + 1) * 512].bitcast(BF16)
                                for m in range(8):
                                    pe(lambda m=m: nc.tensor.transpose(out=pv[:, m * 128:(m + 1) * 128], in_=xsb[:, m * 128:(m + 1) * 128], identity=identb[:]),
                                       [xsb, identb], [bank[ti]], signal=(m == 7))
                                for m in range(8):
                                    act(lambda m=m: nc.scalar.activation(out=hT8[:, m, ti * 128:(ti + 1) * 128], in_=pv[:, m * 128:(m + 1) * 128], func=AF.Identity,
                                                                         scale=s1p[:, m, j:j + 1], bias=modT[:, m, j:j + 1]), [bank[ti], s1p, modT], [hT8])

                        for tb in range(L // TB):
                            t0 = tb * TB
                            chk("blk%d_%d" % (j, tb))
                            if tb == 0:
                                build_hx(0)
                            chk("pa")
                            for m in range(8):
                                if m == 4:
                                    for ti in range(4):
                                        C.dma("sp", xm[ti][:], x2[j, t0 + ti * 128: t0 + (ti + 1) * 128, :], writes=[xm[ti]])
                                if m == 3:
                                    for hf in range(2):
                                        C.dma("sp", wbig[hf][:, 0:8, :], wo_b[hf], reads=[scr["wo"]], writes=[wbig[hf]])
                                wb_bd, wb_gd, wb_bn, wb_gn = next_wc(), next_wc(), next_wc(), next_wc()
                                C.dma("sp", wb_bd[:], wbd_b[m], reads=[scr["wbd"]], writes=[wb_bd])
                                C.dma("sp", wb_gd[:], wgt_b[m], reads=[scr["wgt"]], writes=[wb_gd])
                                C.dma("sp", wb_bn[:, 0:4, :], wbn_b[m], reads=[scr["wbn"]], writes=[wb_bn])
                                C.dma("sp", wb_gn[:], wgt_b[8 + m], reads=[scr["wgt"]], writes=[wb_gn])
                                b0 = 4 * (m % 2)
                                sga, sgb, Y = sg[0], sg[1], (tmpa if m % 2 == 0 else tmpn)
                                for k in range(8):
                                    pe(lambda k=k: nc.tensor.matmul(pcol(b0), lhsT=wb_bd[:, k, :], rhs=odT[:, k, t0:t0 + TB], start=(k == 0), stop=(k == 7)),
                                       [wb_bd, odT], [bank[b0]], signal=(k == 7))
                                for k in range(8):
                                    pe(lambda k=k: nc.tensor.matmul(pcol(b0 + 1), lhsT=wb_gd[:, k, :], rhs=hT8[:, k, :], start=(k == 0), stop=(k == 7)),
                                       [wb_gd, hT8], [bank[b0 + 1]], signal=(k == 7))
                                for k in range(4):
                                    pe(lambda k=k: nc.tensor.matmul(pcol(b0 + 2), lhsT=wb_bn[:, k, :], rhs=onT[:, k, t0:t0 + TB], start=(k == 0), stop=(k == 3)),
                                       [wb_bn, onT], [bank[b0 + 2]], signal=(k == 3))
                                for k in range(8):
                                    pe(lambda k=k: nc.tensor.matmul(pcol(b0 + 3), lhsT=wb_gn[:, k, :], rhs=hT8[:, k, :], start=(k == 0), stop=(k == 7)),
                                       [wb_gn, hT8], [bank[b0 + 3]], signal=(k == 7))
                                act(lambda: nc.scalar.activation(out=sga[:], in_=pcol(b0 + 1), func=AF.Sigmoid, bias=bgT[:, m:m + 1]), [bank[b0 + 1], bgT], [sga])
                                act(lambda: nc.scalar.activation(out=sgb[:], in_=pcol(b0 + 3), func=AF.Sigmoid, bias=bgT[:, 8 + m:9 + m]), [bank[b0 + 3], bgT], [sgb])
                                dve(lambda: nc.vector.tensor_tensor(out=Y[:, 0:512], in0=sga[:], in1=pcol(b0), op=ALU.mult), [sga, bank[b0]], [Y])
                                dve(lambda: nc.vector.tensor_tensor(out=Y[:, 512:1024], in0=sgb[:], in1=pcol(b0 + 2), op=ALU.mult), [sgb, bank[b0 + 2]], [Y])
                                dve(lambda: nc.vector.tensor_tensor(out=yT[:, m, :], in0=Y[:, 0:512], in1=Y[:, 512:1024], op=ALU.add), [Y], [yT])
                            chk("pb")
                            mixb = [0, 2, 4, 0]

                            def mix(ti):
                                for hf in range(2):
                                    pb = mixb[ti] + hf
                                    for k in range(8):
                                        pe(lambda k=k: nc.tensor.matmul(pcol(pb), lhsT=yT[:, k, ti * 128:(ti + 1) * 128], rhs=wbig[hf][:, k, :],
                                                                        start=(k == 0), stop=(k == 7)), [yT, wbig[hf]], [bank[pb]], signal=(k == 7))

                            def chain(ti):
                                pb0 = mixb[ti]
                                r = xm[ti]
                                dve(lambda: nc.vector.tensor_tensor(out=tmpa[:], in0=psum[:, pb0 * 512:(pb0 + 2) * 512], in1=g1b[:], op=ALU.mult),
                                    [bank[pb0], bank[pb0 + 1], g1b], [tmpa])
                                isd = dbg is not None and j == 0 and tb == dbg // 4 and ti == dbg % 4
                                if isd:
                                    dump(2, tmpa[:], tmpa, 1024)
                                    for m_ in range(8):
                                        C.dma("pool", dbg_d[3, :, m_ * 128:(m_ + 1) * 128], odT[:, m_, t0 + ti * 128:t0 + (ti + 1) * 128], reads=[odT])
                                    for m_ in range(4):
                                        C.dma("pool", dbg_d[4, :, m_ * 128:(m_ + 1) * 128], onT[:, m_, t0 + ti * 128:t0 + (ti + 1) * 128], reads=[onT])
                                    for m_ in range(8):
                                        C.dma("pool", dbg_d[5, :, m_ * 128:(m_ + 1) * 128], yT[:, m_, ti * 128:(ti + 1) * 128], reads=[yT])
                                dve(lambda: nc.vector.tensor_tensor(out=r[:], in0=r[:], in1=tmpa[:], op=ALU.add), [r, tmpa], [r])
                                ln_stats(r)
                                dve(lambda: nc.vector.tensor_scalar(out=nb[:], in0=mv[:, 0:1], scalar1=rstd[:, 0:1], scalar2=-1.0, op0=ALU.mult, op1=ALU.mult), [mv, rstd], [nb])
                                act(lambda: nc.scalar.activation(out=tmpn[:], in_=r[:], func=AF.Identity, scale=rstd[:, 0:1], bias=nb[:, 0:1]), [r, rstd, nb], [tmpn])
                                pool(lambda: nc.gpsimd.tensor_tensor(out=r[:], in0=tmpn[:], in1=l1g[:], op=ALU.mult), [tmpn, l1g], [r])
                                pool(lambda: nc.gpsimd.tensor_tensor(out=r[:], in0=r[:], in1=l1b[:], op=ALU.add), [r, l1b], [r])
                                if isd:
                                    dump(6, r[:], r, 1024)
                                build_T(None, tmpn, 6, hT8, ti * 128, lambda m: hms[:, m, j:j + 1], lambda m: hmb[:, m, j:j + 1], [hms, hmb], load=False)

                            mix(0)
                            mix(1)
                            mix(2)
                            chain(0)
                            mix(3)
                            chain(1)
                            chain(2)
                            chain(3)
                            chk("pc")
                            for m in range(NM):
                                if m == 5:
                                    for hf in range(2):
                                        C.dma("sp", wbig[hf][:], wfo_b[hf], reads=[scr["wfo"]], writes=[wbig[hf]])
                                wg, wu = next_wc(), next_wc()
                                C.dma("sp", wg[:], wfi_b[m], reads=[scr["wfi"]], writes=[wg])
                                C.dma("sp", wu[:], wfi_b[NM + m], reads=[scr["wfi"]], writes=[wu])
                                pg, pu = 2 * (m % 2), 2 * (m % 2) + 1
                                for k in range(8):
                                    pe(lambda k=k: nc.tensor.matmul(pcol(pg), lhsT=wg[:, k, :], rhs=hT8[:, k, :], start=(k == 0), stop=(k == 7)), [wg, hT8], [bank[pg]], signal=(k == 7))
                                for k in range(8):
                                    pe(lambda k=k: nc.tensor.matmul(pcol(pu), lhsT=wu[:, k, :], rhs=hT8[:, k, :], start=(k == 0), stop=(k == 7)), [wu, hT8], [bank[pu]], signal=(k == 7))
                                s_ = sg[m % 2]
                                act(lambda: nc.scalar.activation(out=s_[:], in_=pcol(pg), func=AF.Silu), [bank[pg]], [s_])
                                dve(lambda m=m: nc.vector.tensor_tensor(out=hT[:, m, :], in0=s_[:], in1=pcol(pu), op=ALU.mult), [s_, bank[pu]], [hT])
                            chk("pd")
                            if tb + 1 < L // TB:
                                build_hx(tb + 1)
                            for hf in range(2):
                                cs = slice(hf * 512, (hf + 1) * 512)
                                for ti in range(4):
                                    pb = 4 + ti
                                    for m in range(NM):
                                        pe(lambda m=m: nc.tensor.matmul(pcol(pb), lhsT=hT[:, m, ti * 128:(ti + 1) * 128], rhs=wbig[hf][:, m, :],
                                                                        start=(m == 0), stop=(m == NM - 1)), [hT, wbig[hf]], [bank[pb]], signal=(m == NM - 1))
                                    r = xm[ti]
                                    dve(lambda: nc.vector.tensor_tensor(out=tmpa[:, cs], in0=pcol(pb), in1=g2b[:, cs], op=ALU.mult), [bank[pb], g2b], [tmpa])
                                    dve(lambda: nc.vector.tensor_tensor(out=r[:, cs], in0=r[:, cs], in1=tmpa[:, cs], op=ALU.add), [r, tmpa], [r])
                                    if hf == 1:
                                        r = xm[ti]
                                        if dbg is not None and j == 0 and tb == dbg // 4 and ti == dbg % 4:
                                            dump(7, r[:], r, 1024)
                                            for m_ in range(NM):
                                                C.dma("pool", dbg_d[8, :, m_ * 128:(m_ + 1) * 128], hT[:, m_, ti * 128:(ti + 1) * 128], reads=[hT])
                                            for m_ in range(8):
                                                C.dma("pool", dbg_d[9, :, m_ * 128:(m_ + 1) * 128], hT8[:, m_, ti * 128:(ti + 1) * 128], reads=[hT8])
                                        ln_stats(r)
                                        dve(lambda: nc.vector.tensor_scalar(out=nb[:], in0=mv[:, 0:1], scalar1=rstd[:, 0:1], scalar2=-1.0, op0=ALU.mult, op1=ALU.mult), [mv, rstd], [nb])
                                        act(lambda: nc.scalar.activation(out=tmpn[:], in_=r[:], func=AF.Identity, scale=rstd[:, 0:1], bias=nb[:, 0:1]), [r, rstd, nb], [tmpn])
                                        pool(lambda: nc.gpsimd.tensor_tensor(out=tmpn[:], in0=tmpn[:], in1=l2g[:], op=ALU.mult), [tmpn, l2g], [tmpn])
                                        pool(lambda: nc.gpsimd.tensor_tensor(out=r[:], in0=tmpn[:], in1=l2b[:], op=ALU.add), [tmpn, l2b], [r])
                                        C.dma("pool", y[j, t0 + ti * 128: t0 + (ti + 1) * 128, :], r[:], reads=[r])
                        C.barrier()
        except _Stop:
            pass
        C.barrier()
        print("program built: %d instructions, %d dmas, %d sems" % (C.nins, C.dma_n, len(C.semh)))
    return nc


_PROG = None


def kernel(**inputs):
    global _PROG
    f = lambda a: np.ascontiguousarray(np.asarray(a, dtype=np.float32))
    x = f(inputs["x"])
    c = f(inputs["c"])
    ctx = f(inputs["ctx"])
    c_ctx = f(inputs["c_ctx"])
    ropec, ropes, perm = _rope_tables()
    shared = {
        "wmod_t": _tile_w(f(inputs["w_mod"])[0], 512),
        "bmodT": _fm(f(inputs["b_mod"])[0]),
        "bmod": f(inputs["b_mod"])[0],
        "win_t": _tile_w(f(inputs["w_in"])[0], 128),
        "bgT": _fm(f(inputs["b_gate"])[0]),
        "lamv": np.ascontiguousarray(np.stack([f(inputs["lam_q1"])[0], f(inputs["lam_k1"])[0], f(inputs["lam_q2"])[0], f(inputs["lam_k2"])[0]])),
        "subln": f(inputs["subln_g"])[0],
        "rpbt": _rpb_tiles(f(inputs["na_rpb"])[0]),
        "wbd_t": _tile_w(f(inputs["w_branch_diff"])[0], 128),
        "wbn_t": _tile_w(f(inputs["w_branch_na"])[0], 128),
        "wo_t": _tile_w(f(inputs["w_out"])[0], 512),
        "ln1g": f(inputs["ln1_g"])[0],
        "ln1b": f(inputs["ln1_b"])[0],
        "ln1gT": _fm(f(inputs["ln1_g"])[0]),
        "ln1bT": _fm(f(inputs["ln1_b"])[0]),
        "wfi_t": _tile_w(f(inputs["w_ffn_in"])[0], 128),
        "wfo_t": _tile_w(f(inputs["w_ffn_out"])[0], 512),
        "ln2g": f(inputs["ln2_g"])[0],
        "ln2b": f(inputs["ln2_b"])[0],
        "identf": np.eye(128, dtype=np.float32),
        "permf": perm,
        "ropec": ropec,
        "ropes": ropes,
        "ropecq": np.ascontiguousarray(ropec * np.float32(0.125)),
        "maskt": _mask_tables(),
    }
    in_maps = []
    for core in range(N_CORES):
        b0 = 2 * core
        cc = np.stack([c[b0], c[b0 + 1], c_ctx])
        ccT = np.ascontiguousarray(cc.reshape(3, 8, 128).transpose(2, 1, 0))
        m = dict(shared)
        m["x2"] = np.ascontiguousarray(x[b0:b0 + 2])
        m["ctx2"] = np.ascontiguousarray(ctx[b0:b0 + 2])
        m["ccT"] = ccT
        in_maps.append(m)
    if _PROG is None:
        _PROG = build_program()
    res = run_bass_kernel_spmd(_PROG, in_maps, core_ids=list(range(N_CORES)))
    out = np.concatenate([np.asarray(r["y"], dtype=np.float32) for r in res.results], axis=0)
    return out
```

```python
import math
from contextlib import ExitStack

import numpy as np
import concourse.bass as bass
import concourse.mybir as mybir
from concourse.bass_utils import run_bass_kernel_spmd

F32 = mybir.dt.float32
BF16 = mybir.dt.bfloat16
AF = mybir.ActivationFunctionType
ALU = mybir.AluOpType
AX = mybir.AxisListType

D = 1024
L = 2048
LC = 256
LK = L + LC
NKT = LK // 128
GRID_W = 64
FFN_H = 2816
NM = FFN_H // 128
ALPHA = 2.0 ** 0.25
LAM_INIT = 0.8 - 0.6 * math.exp(0.0)
EPS = 1e-5
NEG = -30000.0
N_CORES = 8
TB = 512


class Ev:
    __slots__ = ("eng", "k", "v")

    def __init__(self, eng):
        self.eng = eng
        self.k = None
        self.v = None


class Buf:
    __slots__ = ("t", "lw", "rd", "name", "excl")

    def __init__(self, t, name="", excl=False):
        self.t = t
        self.lw = None
        self.rd = {}
        self.name = name
        self.excl = excl

    def __getitem__(self, k):
        return self.t[k]


class Ctx:
    NDS = 24
    EPOCH = 30000

    def __init__(self, nc, es):
        self.nc = nc
        self.es = es
        self.engs = {"pe": nc.tensor, "act": nc.scalar, "dve": nc.vector, "pool": nc.gpsimd, "sp": nc.sync}
        self.semh = []
        self.cur = {}
        self.cnt = {}
        self.open = {}
        self.dirty = {}
        self.last = {}
        self.waited = {e: {} for e in self.engs}
        for e in ("pe", "act", "dve", "pool"):
            self._new_sem(e)
            self.open[e] = Ev(e)
            self.dirty[e] = False
            self.last[e] = None
        self.dma_key = []
        for i in range(self.NDS):
            self.semh.append(es.enter_context(nc.semaphore("dq%d" % i)))
            self.dma_key.append(len(self.semh) - 1)
        self.dma_last = [None] * self.NDS
        self.dma_n = 0
        self.nins = 0
        self.stopped = False

    def _new_sem(self, e):
        h = self.es.enter_context(self.nc.semaphore("s_%s_%d" % (e, len(self.semh))))
        self.semh.append(h)
        self.cur[e] = len(self.semh) - 1
        self.cnt[e] = 0

    def _need(self, eng, ev, kind, dma=False):
        if ev is None:
            return
        if (not dma) and ev.eng == eng and kind != "RAW":
            return
        if ev.v is None:
            raise RuntimeError("dependency on unsignalled op (eng %s)" % ev.eng)
        w = self.waited[eng]
        if w.get(ev.k, 0) >= ev.v:
            return
        self.engs[eng].wait_ge(self.semh[ev.k], ev.v)
        w[ev.k] = ev.v

    def op(self, eng, fn, reads=(), writes=(), signal=True):
        if self.stopped:
            return None
        for b in reads:
            self._need(eng, b.lw, "RAW")
            if b.excl:
                for ev in b.rd.values():
                    self._need(eng, ev, "WAR")
        for b in writes:
            self._need(eng, b.lw, "WAW")
            for ev in b.rd.values():
                self._need(eng, ev, "WAR")
        ins = fn()
        self.nins += 1
        ev = self.open[eng]
        if signal:
            if self.cnt[eng] >= self.EPOCH:
                self._new_sem(eng)
            ins.then_inc(self.semh[self.cur[eng]], 1)
            self.cnt[eng] += 1
            ev.k = self.cur[eng]
            ev.v = self.cnt[eng]
            self.last[eng] = ev
            self.open[eng] = Ev(eng)
            self.dirty[eng] = False
        else:
            self.dirty[eng] = True
        for b in reads:
            b.rd[eng] = ev
        for b in writes:
            b.lw = ev
            b.rd = {}
        return ins

    def dma(self, q, out_ap, in_ap, reads=(), writes=(), **kw):
        if self.stopped:
            return
        for b in reads:
            self._need(q, b.lw, "RAW", dma=True)
        for b in writes:
            self._need(q, b.lw, "WAW", dma=True)
            for ev in b.rd.values():
                self._need(q, ev, "WAR", dma=True)
        n = self.dma_n
        self.dma_n += 1
        slot = n % self.NDS
        rnd = n // self.NDS
        if rnd > 0:
            self._need(q, self.dma_last[slot], "RAW", dma=True)
        k = self.dma_key[slot]
        self.engs[q].dma_start(out=out_ap, in_=in_ap, **kw).then_inc(self.semh[k], 16)
        self.nins += 1
        ev = Ev("dma")
        ev.k = k
        ev.v = 16 * (rnd + 1)
        self.dma_last[slot] = ev
        for b in reads:
            b.rd[("dma", slot)] = ev
        for b in writes:
            b.lw = ev
            b.rd = {}

    def barrier(self):
        if self.stopped:
            return
        for e in ("pe", "act", "dve", "pool"):
            assert not self.dirty[e], "unsignalled tail on %s" % e
        for e in ("pe", "act", "dve", "pool", "sp"):
            for o in ("pe", "act", "dve", "pool"):
                if o != e and self.last[o] is not None:
                    self._need(e, self.last[o], "RAW", dma=True)
            for ev in self.dma_last:
                if ev is not None:
                    self._need(e, ev, "RAW", dma=True)


def _na_tiles(j):
    if 2 <= j <= 13:
        return [(j + d, d, d + 2) for d in range(-2, 3)]
    e = {0: 0, 1: 1, 14: 2, 15: 3}[j]
    base = 0 if j < 2 else 12
    return [(base + s, base + s - j, 5 + 4 * e + s) for s in range(4)]


def _mask_tables():
    nv = 5 + 16
    m = np.full((128, nv, 128), NEG, np.float32)
    kp = np.arange(128)
    krl, kc = kp // 64, kp % 64
    qrl, qc = kp // 64, kp % 64
    cs = np.clip(qc - 8, 0, GRID_W - 16)
    col_ok = (kc[:, None] >= cs[None, :]) & (kc[:, None] < cs[None, :] + 16)
    done = set()
    for j in [5, 0, 1, 14, 15]:
        for (i, d, var) in _na_tiles(j):
            if var in done:
                continue
            done.add(var)
            kr = 2 * i + krl
            qr = 2 * j + qrl
            rs = np.clip(qr - 4, 0, 32 - 8)
            row_ok = (kr[:, None] >= rs[None, :]) & (kr[:, None] < rs[None, :] + 8)
            m[:, var, :] = np.where(row_ok & col_ok, 0.0, NEG)
    return m


def _rpb_tiles(rpb):
    kp = np.arange(128)
    krl, kc = kp // 64, kp % 64
    out = np.empty((128, 8, 7, 128), np.float32)
    ci = np.clip(kc[:, None] - kc[None, :] + 15, 0, 30)
    for d in range(-3, 4):
        ri = np.clip(2 * d + krl[:, None] - krl[None, :] + 7, 0, 14)
        out[:, :, d + 3, :] = np.transpose(rpb[:, ri, ci], (1, 0, 2))
    return np.ascontiguousarray(out.reshape(128, 4, 2, 7, 128).transpose(0, 1, 3, 2, 4))


def _rope_tables():
    t = np.arange(L)
    row = (t // GRID_W).astype(np.float32)
    col = (t % GRID_W).astype(np.float32)
    inv = (10000.0 ** (-np.arange(16, dtype=np.float32) / 16)).astype(np.float32)
    p = np.arange(128)
    axis = (p % 64) // 32
    half = (p % 32) // 16
    f = p % 16
    pos = np.where(axis[:, None] == 0, row[None, :], col[None, :]).astype(np.float32)
    ang = (pos * inv[f][:, None]).astype(np.float32)
    c = np.cos(ang).astype(np.float32)
    s = np.sin(ang).astype(np.float32) * np.where(half == 0, -1.0, 1.0).astype(np.float32)[:, None]
    perm = np.zeros((128, 128), np.float32)
    perm[p, p ^ 16] = 1.0
    return c, s.astype(np.float32), perm


def _tile_w(w, cw):
    K, N = w.shape
    return np.ascontiguousarray(w.reshape(K // 128, 128, N // cw, cw).transpose(2, 1, 0, 3))


def _fm(v):
    return np.ascontiguousarray(v.reshape(-1, 128).T)


class _Stop(Exception):
    pass


def build_program(stop=None, dbg=None):
    nc = bass.Bass("TRN2", target_bir_lowering=False)

    def din(name, shape):
        return nc.dram_tensor(name, list(shape), F32, kind="ExternalInput").ap()

    x2 = din("x2", [2, L, D])
    ctx2 = din("ctx2", [2, LC, D])
    ccT_d = din("ccT", [128, 8, 3])
    wmod_d = din("wmod_t", [12, 128, 8, 512])
    bmodT_d = din("bmodT", [128, 48])
    bmod_d = din("bmod", [6 * D])
    win_d = din("win_t", [52, 128, 8, 128])
    bgT_d = din("bgT", [128, 16])
    lamv_d = din("lamv", [4, 64])
    subln_d = din("subln", [128])
    rpbt_d = din("rpbt", [128, 4, 7, 2, 128])
    wbd_d = din("wbd_t", [8, 128, 8, 128])
    wbn_d = din("wbn_t", [8, 128, 4, 128])
    wo_d = din("wo_t", [2, 128, 8, 512])
    ln1g_d = din("ln1g", [D])
    ln1b_d = din("ln1b", [D])
    ln1gT_d = din("ln1gT", [128, 8])
    ln1bT_d = din("ln1bT", [128, 8])
    wfi_d = din("wfi_t", [44, 128, 8, 128])
    wfo_d = din("wfo_t", [2, 128, NM, 512])
    ln2g_d = din("ln2g", [D])
    ln2b_d = din("ln2b", [D])
    ident_d = din("identf", [128, 128])
    perm_d = din("permf", [128, 128])
    ropec_d = din("ropec", [128, L])
    ropes_d = din("ropes", [128, L])
    ropecq_d = din("ropecq", [128, L])
    mask_d = din("maskt", [128, 21, 128])
    y = nc.dram_tensor("y", [2, L, D], F32, kind="ExternalOutput").ap()
    dbg_d = nc.dram_tensor("dbg", [12, 128, 4096], F32, kind="ExternalOutput").ap() if dbg is not None else None

    def dscr(name, shape):
        return nc.dram_tensor(name, list(shape), BF16, kind="Internal").ap()
    wbd_b = dscr("wbd_b", [8, 128, 8, 128])
    wgt_b = dscr("wgt_b", [16, 128, 8, 128])
    wbn_b = dscr("wbn_b", [8, 128, 4, 128])
    wo_b = dscr("wo_b", [2, 128, 8, 512])
    wfi_b = dscr("wfi_b", [44, 128, 8, 128])
    wfo_b = dscr("wfo_b", [2, 128, NM, 512])
    scr = {k: Buf(None, k) for k in ("wbd", "wgt", "wbn", "wo", "wfi", "wfo")}
    conv = []
    for m_ in range(8):
        conv.append((wbd_b[m_], wbd_d[m_], scr["wbd"]))
        conv.append((wgt_b[m_], win_d[12 + m_], scr["wgt"]))
        conv.append((wbn_b[m_], wbn_d[m_], scr["wbn"]))
        conv.append((wgt_b[8 + m_], win_d[20 + m_], scr["wgt"]))
    for hf_ in range(2):
        for k0 in range(0, 8, 2):
            conv.append((wo_b[hf_, :, k0:k0 + 2, :], wo_d[hf_, :, k0:k0 + 2, :], scr["wo"]))
    for m_ in range(44):
        conv.append((wfi_b[m_], wfi_d[m_], scr["wfi"]))
    for hf_ in range(2):
        for m0 in range(0, NM, 2):
            conv.append((wfo_b[hf_, :, m0:m0 + 2, :], wfo_d[hf_, :, m0:m0 + 2, :], scr["wfo"]))

    with ExitStack() as es:
        C = Ctx(nc, es)

        def dump(slot, ap, buf, n):
            C.dma("pool", dbg_d[slot, :, 0:n], ap, reads=[buf])

        def chk(name):
            if stop == name and not C.stopped:
                C.barrier()
                C.stopped = True

        try:
            uid = [0]

            def sb(st, name, shape, dt):
                uid[0] += 1
                nm = "sb%d_%s" % (uid[0], name)
                return Buf(st.enter_context(nc.sbuf_tensor(nm, list(shape), dt)), nm)

            def act(fn, reads, writes):
                return C.op("act", fn, reads, writes)

            def dve(fn, reads, writes):
                return C.op("dve", fn, reads, writes)

            def pe(fn, reads, writes, signal=True):
                return C.op("pe", fn, reads, writes, signal)

            def pool(fn, reads, writes):
                return C.op("pool", fn, reads, writes)

            psum = es.enter_context(nc.psum_tensor("psum", [128, 4096], F32))
            bank = [Buf(psum, "bank%d" % i, excl=True) for i in range(8)]

            def pcol(b, lo=0, hi=512):
                return psum[:, b * 512 + lo:b * 512 + hi]

            ident = sb(es, "ident", [128, 128], F32)
            identb = sb(es, "identb", [128, 128], BF16)
            permb = sb(es, "permb", [128, 128], BF16)
            C.dma("sp", ident[:], ident_d, writes=[ident])
            C.dma("pool", identb[:], ident_d, writes=[identb])
            C.dma("pool", permb[:], perm_d, writes=[permb])
            lv = sb(es, "lv", [128, 4, 64], F32)
            C.dma("sp", lv[:], lamv_d.partition_broadcast(128), writes=[lv])
            lpr = sb(es, "lpr", [128, 2, 64], F32)
            lsum = sb(es, "lsum", [128, 2], F32)
            lexp = sb(es, "lexp", [128, 2], F32)
            lamneg = sb(es, "lamneg", [128, 1], F32)
            dve(lambda: nc.vector.tensor_tensor(out=lpr[:, 0, :], in0=lv[:, 0, :], in1=lv[:, 1, :], op=ALU.mult), [lv], [lpr])
            dve(lambda: nc.vector.tensor_tensor(out=lpr[:, 1, :], in0=lv[:, 2, :], in1=lv[:, 3, :], op=ALU.mult), [lv], [lpr])
            dve(lambda: nc.vector.reduce_sum(out=lsum[:], in_=lpr[:], axis=AX.X), [lpr], [lsum])
            act(lambda: nc.scalar.activation(out=lexp[:], in_=lsum[:], func=AF.Exp), [lsum], [lexp])
            dve(lambda: nc.vector.tensor_tensor(out=lamneg[:], in0=lexp[:, 1:2], in1=lexp[:, 0:1], op=ALU.subtract), [lexp], [lamneg])
            dve(lambda: nc.vector.tensor_scalar_add(out=lamneg[:], in0=lamneg[:], scalar1=-LAM_INIT), [lamneg], [lamneg])
            epsL = sb(es, "epsL", [128, 1], F32)
            dve(lambda: nc.vector.memset(epsL[:], EPS / (ALPHA * ALPHA)), [], [epsL])
            epsT = sb(es, "epsT", [128, 1], F32)
            dve(lambda: nc.vector.memset(epsT[:], EPS), [], [epsT])
            gsub = sb(es, "gsub", [128, 128], F32)
            C.dma("sp", gsub[:], subln_d.partition_broadcast(128), writes=[gsub])
            dve(lambda: nc.vector.tensor_scalar_mul(out=gsub[:], in0=gsub[:], scalar1=1.0 - LAM_INIT), [gsub], [gsub])
            bgT = sb(es, "bgT", [128, 16], F32)
            C.dma("sp", bgT[:], bgT_d, writes=[bgT])
            ln1gT = sb(es, "ln1gT", [128, 8], F32)
            ln1bT = sb(es, "ln1bT", [128, 8], F32)
            C.dma("sp", ln1gT[:], ln1gT_d, writes=[ln1gT])
            C.dma("sp", ln1bT[:], ln1bT_d, writes=[ln1bT])
            chk("setup")

            ccT = sb(es, "ccT", [128, 8, 3], F32)
            scT = sb(es, "scT", [128, 8, 3], BF16)
            scB = sb(es, "scB", [128, 2, 8, 128], BF16)
            bmT = sb(es, "bmT", [128, 48], F32)
            modT = sb(es, "modT", [128, 48, 3], F32)
            s1p = sb(es, "s1p", [128, 8, 3], F32)
            s2p = sb(es, "s2p", [128, 8, 3], F32)
            hms = sb(es, "hms", [128, 8, 2], F32)
            hmb = sb(es, "hmb", [128, 8, 2], F32)
            C.dma("sp", ccT[:], ccT_d, writes=[ccT])
            C.dma("sp", bmT[:], bmodT_d, writes=[bmT])
            act(lambda: nc.scalar.activation(out=scT[:], in_=ccT[:], func=AF.Silu), [ccT], [scT])
            for j in range(2):
                dve(lambda j=j: nc.vector.tensor_copy(out=scB[:, j, :, :], in_=scT[:, :, j:j + 1].to_broadcast([128, 8, 128])), [scT], [scB])
            with ExitStack() as ph:
                wm = [sb(ph, "wm%d" % i, [128, 8, 512], BF16) for i in range(4)]
                for blk in range(12):
                    w = wm[blk % 4]
                    C.dma("pool", w[:], wmod_d[blk], writes=[w])
                    for c4 in range(4):
                        ch = blk * 4 + c4
                        for k in range(8):
                            pe(lambda k=k, ch=ch, c4=c4, w=w: nc.tensor.matmul(pcol(0, ch * 3, ch * 3 + 3), lhsT=w[:, k, c4 * 128:(c4 + 1) * 128],
                                                                           rhs=scT[:, k, :], start=(k == 0), stop=(k == 7)),
                               [w, scT], [bank[0]], signal=(k == 7))
                dve(lambda: nc.vector.tensor_tensor(out=modT[:], in0=pcol(0, 0, 144).rearrange("p (c j) -> p c j", j=3),
                                                    in1=bmT[:].unsqueeze(2).to_broadcast([128, 48, 3]), op=ALU.add), [bank[0], bmT], [modT])
                dve(lambda: nc.vector.tensor_scalar_add(out=s1p[:], in0=modT[:, 8:16, :], scalar1=1.0), [modT], [s1p])
                dve(lambda: nc.vector.tensor_scalar_add(out=s2p[:], in0=modT[:, 32:40, :], scalar1=1.0), [modT], [s2p])
                dve(lambda: nc.vector.tensor_tensor(out=hms[:], in0=s2p[:, :, 0:2], in1=ln1gT[:].unsqueeze(2).to_broadcast([128, 8, 2]), op=ALU.mult), [s2p, ln1gT], [hms])
                dve(lambda: nc.vector.tensor_tensor(out=hmb[:], in0=s2p[:, :, 0:2], in1=ln1bT[:].unsqueeze(2).to_broadcast([128, 8, 2]), op=ALU.mult), [s2p, ln1bT], [hmb])
                dve(lambda: nc.vector.tensor_tensor(out=hmb[:], in0=hmb[:], in1=modT[:, 24:32, 0:2], op=ALU.add), [hmb, modT], [hmb])
                C.barrier()
            chk("mod")

            def build_T(src_ap, xt, psb, dst, dst_col, scale_ap_fn, bias_ap_fn, extra_reads, load=True):
                if load:
                    C.dma("sp", xt[:], src_ap, writes=[xt])
                for m in range(8):
                    pe(lambda m=m: nc.tensor.transpose(out=psum[:, psb * 512 + m * 128: psb * 512 + (m + 1) * 128], in_=xt[:, m * 128:(m + 1) * 128],
                                                       identity=ident[:]), [xt, ident], [bank[psb + m // 4]], signal=(m == 7))
                for m in range(8):
                    if m < 4 or not load:
                        act(lambda m=m: nc.scalar.activation(out=dst[:, m, dst_col:dst_col + 128], in_=psum[:, psb * 512 + m * 128: psb * 512 + (m + 1) * 128],
                                                             func=AF.Identity, scale=scale_ap_fn(m), bias=bias_ap_fn(m)),
                            [bank[psb + m // 4]] + extra_reads, [dst])
                    else:
                        dve(lambda m=m: nc.vector.tensor_scalar(out=dst[:, m, dst_col:dst_col + 128], in0=psum[:, psb * 512 + m * 128: psb * 512 + (m + 1) * 128],
                                                                scalar1=scale_ap_fn(m), scalar2=bias_ap_fn(m), op0=ALU.mult, op1=ALU.add),
                            [bank[psb + m // 4]] + extra_reads, [dst])

            for j in range(2):
                with ExitStack() as bs:
                    odT = sb(bs, "odT", [128, 8, L], BF16)
                    onT = sb(bs, "onT", [128, 4, L], BF16)
                    with ExitStack() as ph:
                        ropec = sb(ph, "ropec", [128, L], F32)
                        ropes = sb(ph, "ropes", [128, L], F32)
                        rpbt = sb(ph, "rpbt", [128, 4, 7, 2, 128], BF16)
                        maskt = sb(ph, "maskt", [128, 21, 128], BF16)
                        C.dma("sp", ropec[:], ropec_d, writes=[ropec])
                        C.dma("sp", ropes[:], ropes_d, writes=[ropes])
                        ropecq = sb(ph, "ropecq", [128, L], F32)
                        C.dma("sp", ropecq[:], ropecq_d, writes=[ropecq])
                        C.dma("pool", rpbt[:], rpbt_d, writes=[rpbt])
                        C.dma("pool", maskt[:], mask_d, writes=[maskt])
                        hxT = sb(ph, "hxT", [128, 8, LK], BF16)
                        xts = [sb(ph, "xt%d" % i, [128, D], F32) for i in range(2)]
                        for tt in range(NKT):
                            src = x2[j, tt * 128:(tt + 1) * 128, :] if tt < 16 else ctx2[j, (tt - 16) * 128:(tt - 15) * 128, :]
                            cj = j if tt < 16 else 2
                            build_T(src, xts[tt % 2], 2 * (tt % 2), hxT, tt * 128,
                                    lambda m, cj=cj: s1p[:, m, cj:cj + 1], lambda m, cj=cj: modT[:, m, cj:cj + 1], [s1p, modT])
                        chk("hx")
                        wq = sb(ph, "wq", [128, 8, 128], BF16)
                        wk = sb(ph, "wk", [128, 8, 128], BF16)
                        wv = sb(ph, "wv", [128, 8, 128], BF16)
                        qT = sb(ph, "qT", [128, 2 * L], BF16)
                        biasC = sb(ph, "biasC", [128, 5, 5, 2, 128], BF16)
                        kT = sb(ph, "kT", [128, LK], BF16)
                        Vd = sb(ph, "Vd", [128, NKT, 129], BF16)
                        Vn = sb(ph, "Vn", [128, NKT, 2, 65], BF16)
                        rawb = [sb(ph, "rawb%d" % i, [128, 512], BF16) for i in range(2)]
                        rt1 = [sb(ph, "rt1_%d" % i, [128, 512], F32) for i in range(2)]
                        rt2 = [sb(ph, "rt2_%d" % i, [128, 512], F32) for i in range(2)]
                        Pb = [sb(ph, "Pb%d" % i, [128, 1792], BF16) for i in range(2)]
                        Pd = Pb + [sb(ph, "Pb2", [128, 1024], BF16)]
                        rs = sb(ph, "rs", [128, 4, 2], F32)
                        rsl = sb(ph, "rsl", [128, 4, 1], F32)
                        oA = sb(ph, "oA", [128, 4, 128], F32)
                        oB = sb(ph, "oB", [128, 4, 128], F32)
                        ss = sb(ph, "ss", [128, 4, 1], F32)
                        rsn = sb(ph, "rsn", [128, 2, 1], F32)
                        onn = sb(ph, "onn", [128, 2, 64], F32)
                        dve(lambda: nc.vector.memset(Vd[:, :, 128:129], 1.0), [], [Vd])
                        dve(lambda: nc.vector.memset(Vn[:, :, :, 64:65], 1.0), [], [Vn])
                        chk("ms")

                        def proj_fm(wb, blk_cols, ntok, pb):
                            for k in range(8):
                                pe(lambda k=k: nc.tensor.matmul(pcol(pb, 0, ntok), lhsT=wb[:, k, :], rhs=hxT[:, k, blk_cols:blk_cols + ntok],
                                                                start=(k == 0), stop=(k == 7)), [wb, hxT], [bank[pb]], signal=(k == 7))

                        for u in range(12):
                            chk("ustart%d" % u)
                            diff = u < 8
                            cq, ck, cv = (u, 28 + u, 36 + u) if diff else (8 + (u - 8), 44 + (u - 8), 48 + (u - 8))
                            if u == 8:
                                dve(lambda: nc.vector.memset(qT[:], 0.0), [], [qT])
                            if not diff:
                                for cls in range(5):
                                    tl = _na_tiles([5, 0, 1, 14, 15][cls])
                                    nl_ = len(tl)
                                    d0, v0 = tl[0][1] + 3, tl[0][2]
                                    dve(lambda: nc.vector.tensor_tensor(out=biasC[:, cls, 0:nl_, :, :], in0=rpbt[:, u - 8, d0:d0 + nl_, :, :],
                                                                        in1=maskt[:, v0:v0 + nl_, :].unsqueeze(2).to_broadcast([128, nl_, 2, 128]), op=ALU.add),
                                        [rpbt, maskt], [biasC])
                            C.dma("pool", wq[:], win_d[cq], writes=[wq])
                            C.dma("pool", wk[:], win_d[ck], writes=[wk])
                            C.dma("pool", wv[:], win_d[cv], writes=[wv])
                            if j == 0:
                                npc = (len(conv) + 11 - u) // (12 - u) if u < 11 else len(conv)
                                for _ in range(min(npc, len(conv))):
                                    d_ap, s_ap, sbuf_ = conv.pop(0)
                                    C.dma("pool", d_ap, s_ap, writes=[sbuf_])
                            qbd = qT[:].rearrange("p (j h q) -> p j h q", h=2, q=128)
                            blocks = [("q", b_, 512) for b_ in range(4)] + [("k", b_, 512 if b_ < 4 else 256) for b_ in range(5)]
                            pend = None
                            for idx, (kind, b_, ntok) in enumerate(blocks):
                                pb = idx % 3
                                proj_fm(wq if kind == "q" else wk, b_ * 512, ntok, pb)
                                rope = diff and ntok == 512
                                if rope:
                                    rb = rawb[idx % 2]
                                    if kind == "q":
                                        act(lambda: nc.scalar.mul(out=rb[:], in_=pcol(pb), mul=0.125), [bank[pb]], [rb])
                                    else:
                                        act(lambda: nc.scalar.copy(out=rb[:], in_=pcol(pb)), [bank[pb]], [rb])
                                elif kind == "q":
                                    for hh_ in range(2):
                                        rws = slice(hh_ * 64, (hh_ + 1) * 64)
                                        act(lambda: nc.scalar.mul(out=qbd[rws, b_ * 4:(b_ + 1) * 4, hh_, :],
                                                                  in_=psum[rws, pb * 512:(pb + 1) * 512].rearrange("p (j q) -> p j q", q=128), mul=0.125), [bank[pb]], [qT])
                                else:
                                    act(lambda: nc.scalar.copy(out=kT[:, b_ * 512:b_ * 512 + ntok], in_=pcol(pb, 0, ntok)), [bank[pb]], [kT])

                                def stage2(idx=idx, kind=kind, b_=b_, pb=pb):
                                    pr = 3 + idx % 2
                                    rb = rawb[idx % 2]
                                    a1, a2 = rt1[idx % 2], rt2[idx % 2]
                                    dst = qT if kind == "q" else kT
                                    ctab = ropecq if kind == "q" else ropec
                                    tok0 = b_ * 512
                                    pe(lambda: nc.tensor.matmul(pcol(pr), lhsT=permb[:], rhs=rb[:], start=True, stop=True), [permb, rb], [bank[pr]])
                                    dve(lambda: nc.vector.tensor_tensor(out=a1[:], in0=pcol(pb), in1=ctab[:, tok0:tok0 + 512], op=ALU.mult), [bank[pb], ctab], [a1])
                                    dve(lambda: nc.vector.tensor_tensor(out=a2[:], in0=pcol(pr), in1=ropes[:, tok0:tok0 + 512], op=ALU.mult), [bank[pr], ropes], [a2])
                                    pool(lambda: nc.gpsimd.tensor_tensor(out=dst[:, tok0:tok0 + 512], in0=a1[:], in1=a2[:], op=ALU.add), [a1, a2], [dst])
                                if pend is not None:
                                    pend()
                                pend = stage2 if rope else None
                            if pend is not None:
                                pend()
                            chk("qk%d" % u)
                            for g in range(5):
                                pb = 5 + g % 2
                                nt = 4 if g < 4 else 2
                                for t4 in range(nt):
                                    tt = g * 4 + t4
                                    for k in range(8):
                                        pe(lambda k=k, tt=tt, t4=t4, pb=pb: nc.tensor.matmul(pcol(pb, t4 * 128, (t4 + 1) * 128), lhsT=hxT[:, k, tt * 128:(tt + 1) * 128],
                                                                                           rhs=wv[:, k, :], start=(k == 0), stop=(k == 7)),
                                           [hxT, wv], [bank[pb]], signal=(k == 7 and t4 == nt - 1))
                                if diff:
                                    dve(lambda g=g, nt=nt, pb=pb: nc.vector.tensor_copy(out=Vd[:, g * 4:g * 4 + nt, 0:128],
                                                                                        in_=pcol(pb, 0, nt * 128).rearrange("p (t c) -> p t c", c=128)), [bank[pb]], [Vd])
                                else:
                                    for t4 in range(nt):
                                        dve(lambda g=g, t4=t4, pb=pb: nc.vector.tensor_copy(out=Vn[:, g * 4 + t4, :, 0:64],
                                                                                            in_=pcol(pb, t4 * 128, (t4 + 1) * 128).rearrange("p (h c) -> p h c", c=64)), [bank[pb]], [Vn])
                            chk("proj%d" % u)
                            if diff:
                                h = u
                                Oall = psum[:, 2048:4096].rearrange("p (q c) -> p q c", c=512)
                                obanks = [bank[4], bank[5], bank[6], bank[7]]
                                epi_pend = [None]
                                onbufs = [rt1[0], rt1[1], rt2[0], rt2[1]]
                                for qb in range(4):
                                    def qk(kt, qb=qb):
                                        sbk = (kt % 2) * 2
                                        for hh in range(2):
                                            pe(lambda hh=hh: nc.tensor.matmul(pcol(sbk + hh), lhsT=kT[hh * 64:(hh + 1) * 64, kt * 128:(kt + 1) * 128],
                                                                              rhs=qT[hh * 64:(hh + 1) * 64, qb * 512:(qb + 1) * 512], start=True, stop=True),
                                               [kT, qT], [bank[sbk + hh]], signal=(hh == 1))
                                        act(lambda: nc.scalar.activation(out=Pd[kt % 3][:, 0:1024], in_=psum[:, sbk * 512:(sbk + 2) * 512], func=AF.Exp),
                                            [bank[sbk], bank[sbk + 1]], [Pd[kt % 3]])

                                    def av(kt):
                                        P = Pd[kt % 3]
                                        for qi in range(4):
                                            for hh in range(2):
                                                pe(lambda qi=qi, hh=hh: nc.tensor.matmul(psum[:, (4 + qi) * 512 + hh * 129:(4 + qi) * 512 + hh * 129 + 129],
                                                                                         lhsT=P[:, hh * 512 + qi * 128: hh * 512 + (qi + 1) * 128], rhs=Vd[:, kt, :],
                                                                                         start=(kt == 0 and hh == 0), stop=(kt == NKT - 1)),
                                                   [P, Vd], [bank[4 + qi]], signal=(qi == 3 and hh == 1))
                                    qk(0)
                                    qk(1)
                                    for kt in range(NKT):
                                        if kt + 2 < NKT:
                                            qk(kt + 2)
                                        av(kt)
                                        if kt == 3 and epi_pend[0] is not None:
                                            epi_pend[0]()
                                            epi_pend[0] = None
                                    dve(lambda: nc.vector.reciprocal(out=rs[:], in_=Oall[:, :, 128:258:129]), obanks, [rs])
                                    dve(lambda: nc.vector.tensor_scalar(out=rsl[:], in0=rs[:, :, 1:2], scalar1=lamneg[:, 0:1], scalar2=None, op0=ALU.mult), [rs, lamneg], [rsl])
                                    dve(lambda: nc.vector.tensor_tensor(out=oA[:], in0=Oall[:, :, 0:128], in1=rs[:, :, 0:1].to_broadcast([128, 4, 128]), op=ALU.mult), obanks + [rs], [oA])
                                    dve(lambda: nc.vector.tensor_tensor(out=oB[:], in0=Oall[:, :, 129:257], in1=rsl[:].to_broadcast([128, 4, 128]), op=ALU.mult), obanks + [rsl], [oB])
                                    dve(lambda: nc.vector.tensor_tensor(out=oA[:], in0=oA[:], in1=oB[:], op=ALU.add), [oA, oB], [oA])
                                    dve(lambda: nc.vector.tensor_tensor(out=oB[:], in0=oA[:], in1=oA[:], op=ALU.mult), [oA], [oB])
                                    dve(lambda: nc.vector.reduce_sum(out=ss[:, :, 0], in_=oB[:], axis=AX.X), [oB], [ss])

                                    def epi_tail(qb=qb):
                                        onb = onbufs[qb]
                                        act(lambda: nc.scalar.activation(out=ss[:], in_=ss[:], func=AF.Ln, scale=1.0 / 128, bias=epsT[:, 0:1]), [ss, epsT], [ss])
                                        act(lambda: nc.scalar.activation(out=ss[:], in_=ss[:], func=AF.Exp, scale=-0.5), [ss], [ss])
                                        dve(lambda: nc.vector.tensor_tensor(out=oB[:], in0=oA[:], in1=ss[:].to_broadcast([128, 4, 128]), op=ALU.mult), [oA, ss], [oB])
                                        dve(lambda: nc.vector.tensor_tensor(out=onb[:].rearrange("p (q e) -> p q e", e=128), in0=oB[:],
                                                                            in1=gsub[:].unsqueeze(1).to_broadcast([128, 4, 128]), op=ALU.mult), [oB, gsub], [onb])
                                    epi_pend[0] = epi_tail
                                epi_pend[0]()
                                epi_pend[0] = None
                                for qb in range(4):
                                    onb = onbufs[qb]
                                    for qi in range(4):
                                        pe(lambda: nc.tensor.transpose(out=pcol(4 + qb, qi * 128, (qi + 1) * 128), in_=onb[:, qi * 128:(qi + 1) * 128], identity=ident[:]),
                                           [onb, ident], [bank[4 + qb]], signal=(qi == 3))
                                    act(lambda: nc.scalar.copy(out=odT[:, h, qb * 512:(qb + 1) * 512], in_=pcol(4 + qb)), [bank[4 + qb]], [odT])
                            else:
                                c = u - 8
                                qbd = qT[:].rearrange("p (j h q) -> p j h q", h=2, q=128)
                                def na_qk(jq):
                                    tiles = _na_tiles(jq)
                                    nl = len(tiles)
                                    ntot = nl + 2
                                    cls = {0: 1, 1: 2, 14: 3, 15: 4}.get(jq, 0)
                                    P = Pb[jq % 2]
                                    started = set()
                                    for s0 in range(0, nl, 2):
                                        ns = min(2, nl - s0)
                                        bk = s0 // 2
                                        pe(lambda: nc.tensor.matmul(psum[:, s0 * 256:(s0 + ns) * 256], lhsT=identb[:],
                                                                    rhs=biasC[:, cls, s0:s0 + ns, :, :].rearrange("p s h q -> p (s h q)"), start=True, stop=False),
                                           [identb, biasC], [bank[bk]], signal=False)
                                        started.add(bk)
                                    klist = [t[0] for t in tiles] + [16, 17]
                                    for s_, i in enumerate(klist):
                                        bk = s_ // 2
                                        st_ = bk not in started
                                        started.add(bk)
                                        pe(lambda: nc.tensor.matmul(psum[:, s_ * 256:(s_ + 1) * 256], lhsT=kT[:, i * 128:(i + 1) * 128],
                                                                    rhs=qbd[:, jq, :, :].rearrange("p h q -> p (h q)"), start=st_, stop=True),
                                           [kT, qT], [bank[bk]], signal=(s_ == 3 or s_ == ntot - 1))
                                    act(lambda: nc.scalar.activation(out=P[:, 0:1024], in_=psum[:, 0:1024], func=AF.Exp), [bank[0], bank[1]], [P])
                                    act(lambda: nc.scalar.activation(out=P[:, 1024:ntot * 256], in_=psum[:, 1024:ntot * 256], func=AF.Exp), [bank[2], bank[3]], [P])

                                def na_av(jq):
                                    tiles = _na_tiles(jq)
                                    ntot = len(tiles) + 2
                                    klist = [t[0] for t in tiles] + [16, 17]
                                    ob = 4 + jq % 2
                                    P = Pb[jq % 2]
                                    Pv = P[:].rearrange("p (s h q) -> p s h q", h=2, q=128)
                                    for hh in range(2):
                                        for s_, i in enumerate(klist):
                                            pe(lambda: nc.tensor.matmul(pcol(ob, hh * 65, hh * 65 + 65), lhsT=Pv[:, s_, hh, :], rhs=Vn[:, i, hh, :],
                                                                        start=(s_ == 0), stop=(s_ == ntot - 1)), [P, Vn], [bank[ob]], signal=(s_ == ntot - 1))
                                    dve(lambda: nc.vector.reciprocal(out=rsn[:], in_=pcol(ob, 0, 130).rearrange("p (h c) -> p h c", c=65)[:, :, 64:65]), [bank[ob]], [rsn])
                                    dve(lambda: nc.vector.tensor_tensor(out=onn[:], in0=pcol(ob, 0, 130).rearrange("p (h c) -> p h c", c=65)[:, :, 0:64],
                                                                        in1=rsn[:].to_broadcast([128, 2, 64]), op=ALU.mult), [bank[ob], rsn], [onn])
                                    tb_ = 6 + jq % 2
                                    pe(lambda: nc.tensor.transpose(out=pcol(tb_, 0, 128), in_=onn[:].rearrange("p h c -> p (h c)"), identity=ident[:]), [onn, ident], [bank[tb_]])
                                    act(lambda: nc.scalar.copy(out=onT[:, c, jq * 128:(jq + 1) * 128], in_=pcol(tb_, 0, 128)), [bank[tb_]], [onT])

                                na_qk(0)
                                for jq in range(16):
                                    if jq + 1 < 16:
                                        na_qk(jq + 1)
                                    na_av(jq)
                        C.barrier()

                    with ExitStack() as ph:
                        g1b = sb(ph, "g1b", [128, D], F32)
                        g2b = sb(ph, "g2b", [128, D], F32)
                        l1g = sb(ph, "l1g", [128, D], F32)
                        l1b = sb(ph, "l1b", [128, D], F32)
                        l2g = sb(ph, "l2g", [128, D], F32)
                        l2b = sb(ph, "l2b", [128, D], F32)
                        C.dma("sp", l1g[:], ln1g_d.partition_broadcast(128), writes=[l1g])
                        C.dma("sp", l1b[:], ln1b_d.partition_broadcast(128), writes=[l1b])
                        C.dma("sp", l2g[:], ln2g_d.partition_broadcast(128), writes=[l2g])
                        C.dma("sp", l2b[:], ln2b_d.partition_broadcast(128), writes=[l2b])
                        wbig = [sb(ph, "wbig%d" % i, [128, NM, 512], BF16) for i in range(2)]
                        wc = [sb(ph, "wc%d" % i, [128, 8, 128], BF16) for i in range(6)]
                        xs = [sb(ph, "xs%d" % i, [128, D], BF16) for i in range(2)]
                        wci = [0]

                        def next_wc():
                            b = wc[wci[0] % 6]
                            wci[0] += 1
                            return b
                        tmpa = sb(ph, "tmpa", [128, D], F32)
                        tmpn = sb(ph, "tmpn", [128, D], F32)
                        for gi, gb in ((2, g1b), (5, g2b)):
                            for hf in range(2):
                                w = wbig[hf]
                                C.dma("pool", w[:, 0:8, :], wmod_d[gi * 2 + hf], writes=[w])
                                C.dma("sp", tmpa[:, 0:512], bmod_d[gi * D + hf * 512: gi * D + (hf + 1) * 512].partition_broadcast(128), writes=[tmpa])
                                for k in range(8):
                                    pe(lambda k=k, w=w: nc.tensor.matmul(pcol(0), lhsT=scB[:, j, k, :], rhs=w[:, k, :], start=(k == 0), stop=(k == 7)),
                                       [scB, w], [bank[0]], signal=(k == 7))
                                dve(lambda gb=gb, hf=hf: nc.vector.tensor_tensor(out=gb[:, hf * 512:(hf + 1) * 512], in0=pcol(0), in1=tmpa[:, 0:512], op=ALU.add),
                                    [bank[0], tmpa], [gb])
                            dve(lambda gb=gb: nc.vector.tensor_scalar_mul(out=gb[:], in0=gb[:], scalar1=1.0 / ALPHA), [gb], [gb])
                        chk("gb")
                        if dbg is not None and j == 0:
                            dump(0, g1b[:], g1b, 1024)
                            dump(1, g2b[:], g2b, 1024)
                        xm = [sb(ph, "xm%d" % i, [128, D], F32) for i in range(4)]
                        hT8 = sb(ph, "hT8", [128, 8, TB], BF16)
                        yT = sb(ph, "yT", [128, 8, TB], BF16)
                        hT = sb(ph, "hT", [128, NM, TB], BF16)
                        sg = [sb(ph, "sg%d" % i, [128, TB], F32) for i in range(2)]
                        st = sb(ph, "st", [128, 2, 6], F32)
                        mv = sb(ph, "mv", [128, 2], F32)
                        rstd = sb(ph, "rstd", [128, 1], F32)
                        nb = sb(ph, "nb", [128, 1], F32)

                        def ln_stats(r):
                            for hf in range(2):
                                dve(lambda hf=hf: nc.vector.bn_stats(out=st[:, hf, :], in_=r[:, hf * 512:(hf + 1) * 512]), [r], [st])
                            dve(lambda: nc.vector.bn_aggr(out=mv[:], in_=st[:].rearrange("p a b -> p (a b)")), [st], [mv])
                            act(lambda: nc.scalar.activation(out=rstd[:], in_=mv[:, 1:2], func=AF.Ln, bias=epsL[:, 0:1]), [mv, epsL], [rstd])
                            act(lambda: nc.scalar.activation(out=rstd[:], in_=rstd[:], func=AF.Exp, scale=-0.5), [rstd], [rstd])

                        def build_hx(tbn):
                            for ti in range(4):
                                xsb = xs[ti % 2]
                                C.dma("pool", xsb[:], x2[j, tbn * TB + ti * 128: tbn * TB + (ti + 1) * 128, :], writes=[xsb])
                                pv = psum[:, ti * 512:(ti + 1) * 512].bitcast(BF16)
                                for m in range(8):
                                    pe(lambda m=m: nc.tensor.transpose(out=pv[:, m * 128:(m + 1) * 128], in_=xsb[:, m * 128:(m + 1) * 128], identity=identb[:]),
                                       [xsb, identb], [bank[ti]], signal=(m == 7))
                                for m in range(8):
                                    act(lambda m=m: nc.scalar.activation(out=hT8[:, m, ti * 128:(ti + 1) * 128], in_=pv[:, m * 128:(m + 1) * 128], func=AF.Identity,
                                                                         scale=s1p[:, m, j:j + 1], bias=modT[:, m, j:j + 1]), [bank[ti], s1p, modT], [hT8])

                        for tb in range(L // TB):
                            t0 = tb * TB
                            chk("blk%d_%d" % (j, tb))
                            if tb == 0:
                                build_hx(0)
                            chk("pa")
                            for m in range(8):
                                if m == 4:
                                    for ti in range(4):
                                        C.dma("sp", xm[ti][:], x2[j, t0 + ti * 128: t0 + (ti + 1) * 128, :], writes=[xm[ti]])
                                if m == 3:
                                    for hf in range(2):
                                        C.dma("sp", wbig[hf][:, 0:8, :], wo_b[hf], reads=[scr["wo"]], writes=[wbig[hf]])
                                wb_bd, wb_gd, wb_bn, wb_gn = next_wc(), next_wc(), next_wc(), next_wc()
                                C.dma("sp", wb_bd[:], wbd_b[m], reads=[scr["wbd"]], writes=[wb_bd])
                                C.dma("sp", wb_gd[:], wgt_b[m], reads=[scr["wgt"]], writes=[wb_gd])
                                C.dma("sp", wb_bn[:, 0:4, :], wbn_b[m], reads=[scr["wbn"]], writes=[wb_bn])
                                C.dma("sp", wb_gn[:], wgt_b[8 + m], reads=[scr["wgt"]], writes=[wb_gn])
                                b0 = 4 * (m % 2)
                                sga, sgb, Y = sg[0], sg[1], (tmpa if m % 2 == 0 else tmpn)
                                for k in range(8):
                                    pe(lambda k=k: nc.tensor.matmul(pcol(b0), lhsT=wb_bd[:, k, :], rhs=odT[:, k, t0:t0 + TB], start=(k == 0), stop=(k == 7)),
                                       [wb_bd, odT], [bank[b0]], signal=(k == 7))
                                for k in range(8):
                                    pe(lambda k=k: nc.tensor.matmul(pcol(b0 + 1), lhsT=wb_gd[:, k, :], rhs=hT8[:, k, :], start=(k == 0), stop=(k == 7)),
                                       [wb_gd, hT8], [bank[b0 + 1]], signal=(k == 7))
                                for k in range(4):
                                    pe(lambda k=k: nc.tensor.matmul(pcol(b0 + 2), lhsT=wb_bn[:, k, :], rhs=onT[:, k, t0:t0 + TB], start=(k == 0), stop=(k == 3)),
                                       [wb_bn, onT], [bank[b0 + 2]], signal=(k == 3))
                                for k in range(8):
                                    pe(lambda k=k: nc.tensor.matmul(pcol(b0 + 3), lhsT=wb_gn[:, k, :], rhs=hT8[:, k, :], start=(k == 0), stop=(k == 7)),
                                       [wb_gn, hT8], [bank[b0 + 3]], signal=(k == 7))
                                act(lambda: nc.scalar.activation(out=sga[:], in_=pcol(b0 + 1), func=AF.Sigmoid, bias=bgT[:, m:m + 1]), [bank[b0 + 1], bgT], [sga])
                                act(lambda: nc.scalar.activation(out=sgb[:], in_=pcol(b0 + 3), func=AF.Sigmoid, bias=bgT[:, 8 + m:9 + m]), [bank[b0 + 3], bgT], [sgb])
                                dve(lambda: nc.vector.tensor_tensor(out=Y[:, 0:512], in0=sga[:], in1=pcol(b0), op=ALU.mult), [sga, bank[b0]], [Y])
                                dve(lambda: nc.vector.tensor_tensor(out=Y[:, 512:1024], in0=sgb[:], in1=pcol(b0 + 2), op=ALU.mult), [sgb, bank[b0 + 2]], [Y])
                                dve(lambda: nc.vector.tensor_tensor(out=yT[:, m, :], in0=Y[:, 0:512], in1=Y[:, 512:1024], op=ALU.add), [Y], [yT])
                            chk("pb")
                            mixb = [0, 2, 4, 0]

                            def mix(ti):
                                for hf in range(2):
                                    pb = mixb[ti] + hf
                                    for k in range(8):
                                        pe(lambda k=k: nc.tensor.matmul(pcol(pb), lhsT=yT[:, k, ti * 128:(ti + 1) * 128], rhs=wbig[hf][:, k, :],
                                                                        start=(k == 0), stop=(k == 7)), [yT, wbig[hf]], [bank[pb]], signal=(k == 7))

                            def chain(ti):
                                pb0 = mixb[ti]
                                r = xm[ti]
                                dve(lambda: nc.vector.tensor_tensor(out=tmpa[:], in0=psum[:, pb0 * 512:(pb0 + 2) * 512], in1=g1b[:], op=ALU.mult),
                                    [bank[pb0], bank[pb0 + 1], g1b], [tmpa])
                                isd = dbg is not None and j == 0 and tb == dbg // 4 and ti == dbg % 4
                                if isd:
                                    dump(2, tmpa[:], tmpa, 1024)
                                    for m_ in range(8):
                                        C.dma("pool", dbg_d[3, :, m_ * 128:(m_ + 1) * 128], odT[:, m_, t0 + ti * 128:t0 + (ti + 1) * 128], reads=[odT])
                                    for m_ in range(4):
                                        C.dma("pool", dbg_d[4, :, m_ * 128:(m_ + 1) * 128], onT[:, m_, t0 + ti * 128:t0 + (ti + 1) * 128], reads=[onT])
                                    for m_ in range(8):
                                        C.dma("pool", dbg_d[5, :, m_ * 128:(m_ + 1) * 128], yT[:, m_, ti * 128:(ti + 1) * 128], reads=[yT])
                                dve(lambda: nc.vector.tensor_tensor(out=r[:], in0=r[:], in1=tmpa[:], op=ALU.add), [r, tmpa], [r])
                                ln_stats(r)
                                dve(lambda: nc.vector.tensor_scalar(out=nb[:], in0=mv[:, 0:1], scalar1=rstd[:, 0:1], scalar2=-1.0, op0=ALU.mult, op1=ALU.mult), [mv, rstd], [nb])
                                act(lambda: nc.scalar.activation(out=tmpn[:], in_=r[:], func=AF.Identity, scale=rstd[:, 0:1], bias=nb[:, 0:1]), [r, rstd, nb], [tmpn])
                                pool(lambda: nc.gpsimd.tensor_tensor(out=r[:], in0=tmpn[:], in1=l1g[:], op=ALU.mult), [tmpn, l1g], [r])
                                pool(lambda: nc.gpsimd.tensor_tensor(out=r[:], in0=r[:], in1=l1b[:], op=ALU.add), [r, l1b], [r])
                                if isd:
                                    dump(6, r[:], r, 1024)
                                build_T(None, tmpn, 6, hT8, ti * 128, lambda m: hms[:, m, j:j + 1], lambda m: hmb[:, m, j:j + 1], [hms, hmb], load=False)

                            mix(0)
                            mix(1)
                            mix(2)
                            chain(0)
                            mix(3)
                            chain(1)
                            chain(2)
                            chain(3)
                            chk("pc")
                            for m in range(NM):
                                if m == 5:
                                    for hf in range(2):
                                        C.dma("sp", wbig[hf][:], wfo_b[hf], reads=[scr["wfo"]], writes=[wbig[hf]])
                                wg, wu = next_wc(), next_wc()
                                C.dma("sp", wg[:], wfi_b[m], reads=[scr["wfi"]], writes=[wg])
                                C.dma("sp", wu[:], wfi_b[NM + m], reads=[scr["wfi"]], writes=[wu])
                                pg, pu = 2 * (m % 2), 2 * (m % 2) + 1
                                for k in range(8):
                                    pe(lambda k=k: nc.tensor.matmul(pcol(pg), lhsT=wg[:, k, :], rhs=hT8[:, k, :], start=(k == 0), stop=(k == 7)), [wg, hT8], [bank[pg]], signal=(k == 7))
                                for k in range(8):
                                    pe(lambda k=k: nc.tensor.matmul(pcol(pu), lhsT=wu[:, k, :], rhs=hT8[:, k, :], start=(k == 0), stop=(k == 7)), [wu, hT8], [bank[pu]], signal=(k == 7))
                                s_ = sg[m % 2]
                                act(lambda: nc.scalar.activation(out=s_[:], in_=pcol(pg), func=AF.Silu), [bank[pg]], [s_])
                                dve(lambda m=m: nc.vector.tensor_tensor(out=hT[:, m, :], in0=s_[:], in1=pcol(pu), op=ALU.mult), [s_, bank[pu]], [hT])
                            chk("pd")
                            if tb + 1 < L // TB:
                                build_hx(tb + 1)
                            for hf in range(2):
                                cs = slice(hf * 512, (hf + 1) * 512)
                                for ti in range(4):
                                    pb = 4 + ti
                                    for m in range(NM):
                                        pe(lambda m=m: nc.tensor.matmul(pcol(pb), lhsT=hT[:, m, ti * 128:(ti + 1) * 128], rhs=wbig[hf][:, m, :],
                                                                        start=(m == 0), stop=(m == NM - 1)), [hT, wbig[hf]], [bank[pb]], signal=(m == NM - 1))
                                    r = xm[ti]
                                    dve(lambda: nc.vector.tensor_tensor(out=tmpa[:, cs], in0=pcol(pb), in1=g2b[:, cs], op=ALU.mult), [bank[pb], g2b], [tmpa])
                                    dve(lambda: nc.vector.tensor_tensor(out=r[:, cs], in0=r[:, cs], in1=tmpa[:, cs], op=ALU.add), [r, tmpa], [r])
                                    if hf == 1:
                                        r = xm[ti]
                                        if dbg is not None and j == 0 and tb == dbg // 4 and ti == dbg % 4:
                                            dump(7, r[:], r, 1024)
                                            for m_ in range(NM):
                                                C.dma("pool", dbg_d[8, :, m_ * 128:(m_ + 1) * 128], hT[:, m_, ti * 128:(ti + 1) * 128], reads=[hT])
                                            for m_ in range(8):
                                                C.dma("pool", dbg_d[9, :, m_ * 128:(m_ + 1) * 128], hT8[:, m_, ti * 128:(ti + 1) * 128], reads=[hT8])
                                        ln_stats(r)
                                        dve(lambda: nc.vector.tensor_scalar(out=nb[:], in0=mv[:, 0:1], scalar1=rstd[:, 0:1], scalar2=-1.0, op0=ALU.mult, op1=ALU.mult), [mv, rstd], [nb])
                                        act(lambda: nc.scalar.activation(out=tmpn[:], in_=r[:], func=AF.Identity, scale=rstd[:, 0:1], bias=nb[:, 0:1]), [r, rstd, nb], [tmpn])
                                        pool(lambda: nc.gpsimd.tensor_tensor(out=tmpn[:], in0=tmpn[:], in1=l2g[:], op=ALU.mult), [tmpn, l2g], [tmpn])
                                        pool(lambda: nc.gpsimd.tensor_tensor(out=r[:], in0=tmpn[:], in1=l2b[:], op=ALU.add), [tmpn, l2b], [r])
                                        C.dma("pool", y[j, t0 + ti * 128: t0 + (ti + 1) * 128, :], r[:], reads=[r])
                        C.barrier()
        except _Stop:
            pass
        C.barrier()
        print("program built: %d instructions, %d dmas, %d sems" % (C.nins, C.dma_n, len(C.semh)))
    return nc


_PROG = None


def kernel(**inputs):
    global _PROG
    f = lambda a: np.ascontiguousarray(np.asarray(a, dtype=np.float32))
    x = f(inputs["x"])
    c = f(inputs["c"])
    ctx = f(inputs["ctx"])
    c_ctx = f(inputs["c_ctx"])
    ropec, ropes, perm = _rope_tables()
    shared = {
        "wmod_t": _tile_w(f(inputs["w_mod"])[0], 512),
        "bmodT": _fm(f(inputs["b_mod"])[0]),
        "bmod": f(inputs["b_mod"])[0],
        "win_t": _tile_w(f(inputs["w_in"])[0], 128),
        "bgT": _fm(f(inputs["b_gate"])[0]),
        "lamv": np.ascontiguousarray(np.stack([f(inputs["lam_q1"])[0], f(inputs["lam_k1"])[0], f(inputs["lam_q2"])[0], f(inputs["lam_k2"])[0]])),
        "subln": f(inputs["subln_g"])[0],
        "rpbt": _rpb_tiles(f(inputs["na_rpb"])[0]),
        "wbd_t": _tile_w(f(inputs["w_branch_diff"])[0], 128),
        "wbn_t": _tile_w(f(inputs["w_branch_na"])[0], 128),
        "wo_t": _tile_w(f(inputs["w_out"])[0], 512),
        "ln1g": f(inputs["ln1_g"])[0],
        "ln1b": f(inputs["ln1_b"])[0],
        "ln1gT": _fm(f(inputs["ln1_g"])[0]),
        "ln1bT": _fm(f(inputs["ln1_b"])[0]),
        "wfi_t": _tile_w(f(inputs["w_ffn_in"])[0], 128),
        "wfo_t": _tile_w(f(inputs["w_ffn_out"])[0], 512),
        "ln2g": f(inputs["ln2_g"])[0],
        "ln2b": f(inputs["ln2_b"])[0],
        "identf": np.eye(128, dtype=np.float32),
        "permf": perm,
        "ropec": ropec,
        "ropes": ropes,
        "ropecq": np.ascontiguousarray(ropec * np.float32(0.125)),
        "maskt": _mask_tables(),
    }
    in_maps = []
    for core in range(N_CORES):
        b0 = 2 * core
        cc = np.stack([c[b0], c[b0 + 1], c_ctx])
        ccT = np.ascontiguousarray(cc.reshape(3, 8, 128).transpose(2, 1, 0))
        m = dict(shared)
        m["x2"] = np.ascontiguousarray(x[b0:b0 + 2])
        m["ctx2"] = np.ascontiguousarray(ctx[b0:b0 + 2])
        m["ccT"] = ccT
        in_maps.append(m)
    if _PROG is None:
        _PROG = build_program()
    res = run_bass_kernel_spmd(_PROG, in_maps, core_ids=list(range(N_CORES)))
    out = np.concatenate([np.asarray(r["y"], dtype=np.float32) for r in res.results], axis=0)
    return out
```
